# Optimizing a Trainium2 kernel written in Bass

```python
import jax, jax.numpy as jnp
from jax import lax
import numpy as np

D_MODEL = 1024
BATCH = 4
SEQ = 4096
DEPTH = 4

HEAD_DIM = 64
D_A = D_MODEL // 4
D_B = 3 * D_MODEL // 8
D_C = D_MODEL - D_A - D_B
N_HEADS_A = D_A // HEAD_DIM
N_HEADS_B = D_B // HEAD_DIM
N_HEADS_C = D_C // HEAD_DIM
D_MIX = D_A + D_B + D_C
D_IN = 3 * D_A + 2 * D_B + 2 * D_C
SHORT_CONV = 3
CHUNK = 128
CONF_CONV = 31
FFN_CONV = 3
D_FF = ((8 * D_MODEL // 3) + 255) // 256 * 256
EPS = 1e-6

kernel_name = "hybrid_conv_sgu_conformer_trunk"


def rms_norm(x, g):
    xf = x.astype(jnp.float32)
    y = xf * lax.rsqrt(jnp.mean(xf * xf, axis=-1, keepdims=True) + EPS)
    return (y * g.astype(jnp.float32)).astype(x.dtype)


def layer_norm(x, g, b):
    xf = x.astype(jnp.float32)
    mu = jnp.mean(xf, axis=-1, keepdims=True)
    xc = xf - mu
    y = xc * lax.rsqrt(jnp.mean(xc * xc, axis=-1, keepdims=True) + EPS)
    return (y * g.astype(jnp.float32) + b.astype(jnp.float32)).astype(x.dtype)


def causal_dwconv(x, w):
    K, C = w.shape
    return lax.conv_general_dilated(
        x, w[:, None, :].astype(x.dtype),
        window_strides=(1,), padding=[(K - 1, 0)],
        dimension_numbers=("NWC", "WIO", "NWC"),
        feature_group_count=C)


def short_gated_conv(za, conv_w):
    b_gate, c_gate, xa = jnp.split(za, 3, axis=-1)
    return b_gate * causal_dwconv(c_gate * xa, conv_w)


def chunk_spatial_gate(zb, ln_g, ln_b, w_s, b_s):
    z = jax.nn.gelu(zb, approximate=False)
    u, v = jnp.split(z, 2, axis=-1)
    v = layer_norm(v, ln_g, ln_b)
    bsz, t, _ = v.shape
    v = v.reshape(bsz, t // CHUNK, CHUNK, N_HEADS_B, HEAD_DIM)
    mask = jnp.tril(jnp.ones((CHUNK, CHUNK), dtype=bool))
    w = jnp.where(mask[None], w_s, jnp.zeros((), w_s.dtype))
    s = jnp.einsum("hts,bnshd->bnthd", w, v) + b_s.T[None, None, :, :, None]
    return u * s.reshape(bsz, t, D_B)


def conformer_conv(zc, conv_w, conv_b, ln_g, ln_b):
    a, g = jnp.split(zc, 2, axis=-1)
    y = a * jax.nn.sigmoid(g)
    y = causal_dwconv(y, conv_w) + conv_b
    return jax.nn.silu(layer_norm(y, ln_g, ln_b))


def setup_inputs(seed: int = 0) -> dict:
    key = jax.random.key(seed)
    ks = jax.random.split(key, 24)
    L = DEPTH
    f32 = jnp.float32

    def nrm(k, shape, scale):
        return jax.random.normal(k, shape, f32) * scale

    def gain(k, shape):
        return 1.0 + 0.02 * jax.random.normal(k, shape, f32)

    return {
        "x": jax.random.normal(ks[0], (BATCH, SEQ, D_MODEL), f32),
        "pre_mix_g": gain(ks[1], (L, D_MODEL)),
        "w_in": nrm(ks[2], (L, D_MODEL, D_IN), D_MODEL ** -0.5),
        "conv_a_w": nrm(ks[3], (L, SHORT_CONV, D_A), SHORT_CONV ** -0.5),
        "sgu_ln_g": gain(ks[4], (L, D_B)),
        "sgu_ln_b": nrm(ks[5], (L, D_B), 0.02),
        "sgu_w": nrm(ks[6], (L, N_HEADS_B, CHUNK, CHUNK), CHUNK ** -0.5),
        "sgu_b": gain(ks[7], (L, N_HEADS_B, CHUNK)),
        "conv_c_w": nrm(ks[8], (L, CONF_CONV, D_C), CONF_CONV ** -0.5),
        "conv_c_b": nrm(ks[9], (L, D_C), 0.02),
        "conv_ln_g": gain(ks[10], (L, D_C)),
        "conv_ln_b": nrm(ks[11], (L, D_C), 0.02),
        "grp_norm_g": gain(ks[12], (L, D_MIX)),
        "w_out": nrm(ks[13], (L, D_MIX, D_MODEL), D_MIX ** -0.5),
        "post_mix_g": gain(ks[14], (L, D_MODEL)),
        "pre_ffn_g": gain(ks[15], (L, D_MODEL)),
        "w_up": nrm(ks[16], (L, D_MODEL, 2 * D_FF), D_MODEL ** -0.5),
        "ffn_conv_w": nrm(ks[17], (L, FFN_CONV, 2 * D_FF), FFN_CONV ** -0.5),
        "w_down": nrm(ks[18], (L, D_FF, D_MODEL), D_FF ** -0.5),
        "post_ffn_g": gain(ks[19], (L, D_MODEL)),
    }


def reference(x, pre_mix_g, w_in, conv_a_w, sgu_ln_g, sgu_ln_b, sgu_w, sgu_b,
              conv_c_w, conv_c_b, conv_ln_g, conv_ln_b, grp_norm_g, w_out,
              post_mix_g, pre_ffn_g, w_up, ffn_conv_w, w_down, post_ffn_g):
    for l in range(DEPTH):
        h = rms_norm(x, pre_mix_g[l])
        z = h @ w_in[l]
        za, zb, zc = jnp.split(z, [3 * D_A, 3 * D_A + 2 * D_B], axis=-1)
        ya = short_gated_conv(za, conv_a_w[l])
        yb = chunk_spatial_gate(zb, sgu_ln_g[l], sgu_ln_b[l], sgu_w[l], sgu_b[l])
        yc = conformer_conv(zc, conv_c_w[l], conv_c_b[l], conv_ln_g[l], conv_ln_b[l])
        ga, gb, gc = jnp.split(grp_norm_g[l], [D_A, D_A + D_B])
        y = jnp.concatenate([rms_norm(ya, ga), rms_norm(yb, gb), rms_norm(yc, gc)], axis=-1)
        x = x + rms_norm(y @ w_out[l], post_mix_g[l])
        h = rms_norm(x, pre_ffn_g[l])
        up = causal_dwconv(h @ w_up[l], ffn_conv_w[l])
        gate, val = jnp.split(up, 2, axis=-1)
        x = x + rms_norm((jax.nn.silu(gate) * val) @ w_down[l], post_ffn_g[l])
    return x
```

```python
import numpy as np
from contextlib import ExitStack
import concourse.bass as bass
import concourse.mybir as mybir
from concourse.bass_utils import run_bass_kernel_spmd

F32 = mybir.dt.float32
BF16 = mybir.dt.bfloat16
AF = mybir.ActivationFunctionType
ALU = mybir.AluOpType

D = 1024
KT = 8
SEQ = 4096
BATCH = 4
DEPTH = 4
HALO = 256
T = SEQ // 2 + HALO
N = 384
NTILES = T // N
DFF = 2816
NFF = DFF // 128
EPS = 1e-6
NSLOT = 4
CH = 2048
NCHUNK = 9 + 4 + 22 + 12
NP = 280
NTMP = 6
DSCHED = [(0, 0), (0, 1), (1, 0), (1, 1), (0, 2), (1, 2), (2, 0), (2, 1), (2, 2), (3, 0), (3, 1), (3, 2)]

C_PRE, C_GRP, C_POST, C_PFF, C_OFF = 0, 8, 16, 24, 32
C_CA, C_CC, C_CB, C_LG, C_LB, C_FG, C_FV = 40, 46, 139, 142, 145, 148, 214


class Buf:
    __slots__ = ("name", "w", "r")

    def __init__(self, name):
        self.name = name
        self.w = {}
        self.r = {}


class Prog:
    ENGS = ("pe", "act", "dve", "pool", "sp")

    def __init__(self):
        self.ops = {e: [] for e in self.ENGS}
        self.cnt = {e: 0 for e in self.ENGS}
        self.waited = {e: {} for e in self.ENGS}
        self.dmacnt = {}
        self.pelog = []
        self.npe = 0

    def _deps(self, eng, reads, writes):
        deps = {}
        for b in reads:
            for k, v in b.w.items():
                if deps.get(k, 0) < v:
                    deps[k] = v
        for b in writes:
            for k, v in b.w.items():
                if k != eng and deps.get(k, 0) < v:
                    deps[k] = v
            for k, v in b.r.items():
                if k != eng and deps.get(k, 0) < v:
                    deps[k] = v
        wd = self.waited[eng]
        for k, v in deps.items():
            if wd.get(k, 0) < v:
                self.ops[eng].append(("wait", k, v))
                wd[k] = v

    @staticmethod
    def _mark(key, val, reads, writes):
        for b in reads:
            if b.r.get(key, 0) < val:
                b.r[key] = val
        for b in writes:
            if b.r:
                b.w = {key: val}
                b.r = {}
            else:
                b.w[key] = val

    def group(self, eng, fns, reads=(), writes=()):
        if eng == "pe":
            self.pelog.append((self.npe, len(fns), [b.name for b in reads], [b.name for b in writes]))
            self.npe += len(fns)
        self._deps(eng, reads, writes)
        self.cnt[eng] += 1
        c = self.cnt[eng]
        for f in fns[:-1]:
            self.ops[eng].append(("ins", f, False))
        self.ops[eng].append(("ins", fns[-1], True))
        self._mark(eng, c, reads, writes)

    def op(self, eng, fn, reads=(), writes=()):
        self.group(eng, [fn], reads, writes)

    def dma_batch(self, semkey, items, q="sp"):
        for fn, reads, writes in items:
            self._deps(q, reads, writes)
        total = self.dmacnt.get(semkey, 0) + 16 * len(items)
        self.dmacnt[semkey] = total
        for fn, reads, writes in items:
            self.ops[q].append(("dma", fn, semkey))
            self._mark(semkey, total, reads, writes)

    def dma(self, semkey, fn, reads=(), writes=(), q="sp"):
        self.dma_batch(semkey, [(fn, reads, writes)], q=q)


def _mm(out, lhsT, rhs, start, stop):
    return lambda e: e.matmul(out, lhsT=lhsT, rhs=rhs, start=start, stop=stop)


def build_program(L, layer0=0, ntiles=NTILES, debug=False):
    nc = bass.Bass("TRN2", target_bir_lowering=False)
    TT = ntiles * N
    x_in = nc.dram_tensor("x_in", [128, KT, TT], F32, kind="ExternalInput").ap()
    wst = nc.dram_tensor("wst", [L * NCHUNK, 128, CH], F32, kind="ExternalInput").ap()
    pp_d = nc.dram_tensor("pp", [L, 128, NP], F32, kind="ExternalInput").ap()
    lng_d = nc.dram_tensor("lng", [L, 128, 384], F32, kind="ExternalInput").ap()
    lnb_d = nc.dram_tensor("lnb", [L, 128, 384], F32, kind="ExternalInput").ap()
    wT_d = nc.dram_tensor("sguwT", [L, 128, 6 * 128], F32, kind="ExternalInput").ap()
    bs_d = nc.dram_tensor("sgub", [L, 2, 3 * 128], F32, kind="ExternalInput").ap()
    cst_d = nc.dram_tensor("cst", [128, 256], F32, kind="ExternalInput").ap()
    sel_d = nc.dram_tensor("sel", [2, 128], F32, kind="ExternalInput").ap()
    y_out = nc.dram_tensor("y_out", [128, KT, TT], F32, kind="ExternalOutput").ap()

    if debug:
        dbg = {"d_h": ([128, KT, N], BF16), "d_h2": ([128, KT, N + 2], BF16), "d_y": ([128, KT, N], F32),
               "d_R": ([128, NFF * N], BF16), "d_q": ([128, 3, N + 30], BF16), "d_p": ([128, 2, N + 2], BF16),
               "d_x1": ([128, KT, N], F32), "d_h1": ([128, KT, N], BF16), "d_sq1": ([128, KT, N], BF16), "d_t1": ([128, N], F32),
               "d_vT": ([128, 3, 768], BF16), "d_gs": ([128, 40], F32), "d_wT": ([128, 6, 128], BF16)}
        dbg_d = {k: nc.dram_tensor(k, sh, dt, kind="ExternalOutput").ap() for k, (sh, dt) in dbg.items()}
    P = Prog()
    es = ExitStack()

    def sb(name, shape, dt):
        return es.enter_context(nc.sbuf_tensor("sb_" + name, shape, dt))

    with es:
        x = sb("x", [128, KT, TT], F32)
        ring = [sb(f"ring{i}", [128, CH], F32) for i in range(NSLOT)]
        ones_bf = sb("ones_bf", [128, 128], BF16)
        cst = sb("cst", [128, 256], F32)
        neghalf = sb("neghalf", [128, 8], F32)
        smalls = [sb(f"sm{i}", [128, 8], F32) for i in range(6)]
        dgs = [sb(f"dg{i}", [128, 2, 3, 128], BF16) for i in range(3)]
        hls = [sb(f"hl{i}", [128, 2, 4], BF16) for i in range(3)]
        pp = sb("ppt", [128, NP], F32)
        gs = sb("gs", [128, 40], F32)
        gpre = sb("gpre", [128, L, 8], F32)
        rs_sb = sb("rs_sb", [128, N], F32)
        ccw = sb("ccw", [128, 93], F32)
        diagA = sb("diagA", [128, 6, 128], BF16)
        diagC = sb("diagC", [128, 3, 31, 128], BF16)
        wT = sb("wT", [128, 6, 128], BF16)
        bsr = sb("bsr", [2, 3, 128], BF16)
        sel = sb("sel", [2, 128], BF16)
        lng = sb("lng", [128, 384], F32)
        lnb = sb("lnb", [128, 384], F32)
        h = sb("h", [128, KT, N], BF16)
        sq = sb("sq", [128, KT, N], BF16)
        y = sb("y", [128, KT, N], F32)
        h2 = sb("h2", [128, KT, N + 2], BF16)
        R = sb("R", [128, NFF * N], BF16)
        pbuf = sb("pbuf", [128, 2, N + 2], BF16)
        qbuf = sb("qbuf", [128, 3, N + 30], BF16)
        vTz = sb("vTz", [128, 3, 768], BF16)
        bnst = sb("bnst", [128, 3, 6], F32)
        mv = sb("mv", [128, 3, 2], F32)
        rv = sb("rv", [128, 3, 1], F32)
        dummy = sb("dummy", [128, 2], F32)
        tmps = [sb(f"tmp{i}", [128, N], F32) for i in range(NTMP)]
        banks = [es.enter_context(nc.psum_tensor(f"ps{i}", [128, 512], F32)) for i in range(8)]

        sqflat = sq[:, :, :].rearrange("p k n -> p (k n)")
        wTf = sqflat[:, 0:1536].bitcast(F32).rearrange("p (h t) -> p h t", h=6)
        bsf = sqflat[0:2, 1536:2304].bitcast(F32).rearrange("p (h t) -> p h t", h=3)
        self_ = sqflat[0:2, 2304:2560].bitcast(F32)
        gated = R[:, :].rearrange("p (j n) -> p j n", j=NFF)
        u = R[:, 0:2304].bitcast(F32).rearrange("p (j n) -> p j n", j=3)
        ycp = R[:, 2304:4608].bitcast(F32).rearrange("p (j n) -> p j n", j=3)
        ycb = R[:, 4608:5760].rearrange("p (j n) -> p j n", j=3)
        sqc = R[:, 5760:6912].rearrange("p (j n) -> p j n", j=3)
        ident = cst[:, 0:128]
        maskT = cst[:, 128:256]
        hv = [r[:, :].bitcast(BF16)[:, 1::2] for r in ring]

        X = [[Buf(f"x{t}_{k}") for k in range(KT)] for t in range(ntiles)]
        RING = [Buf(f"ring{i}") for i in range(NSLOT)]
        CONST = Buf("const")
        PPB, GSB, DIAGA, SGUW, LNB = Buf("pp"), Buf("gs"), Buf("diagA"), Buf("sguw"), Buf("ln")
        DIAGC = [Buf("diagC0"), Buf("diagC1"), Buf("diagC2")]
        WTF, BSF = Buf("wTf"), Buf("bsf")
        H, H2 = Buf("h"), [Buf(f"h2_{k}") for k in range(KT)]
        SQ = [Buf(f"sq{i}") for i in range(KT)]
        Y = [Buf(f"y{i}") for i in range(KT)]
        G = [Buf(f"g{i}") for i in range(NFF)]
        PB = [Buf(f"p{i}") for i in range(2)]
        QB = [Buf(f"q{i}") for i in range(3)]
        U = [Buf(f"u{i}") for i in range(3)]
        VT = [Buf(f"vt{i}") for i in range(3)]
        YCP, YCB, SQC = Buf("ycp"), [Buf(f"ycb{i}") for i in range(3)], [Buf(f"sqc{i}") for i in range(3)]
        BN = [Buf(f"bn{i}") for i in range(3)]
        MV = [Buf(f"mv{i}") for i in range(3)]
        RV = [Buf(f"rv{i}") for i in range(3)]
        GUARD = Buf("guard")
        RSSB = Buf("rs_sb")
        CCWB = Buf("ccw")
        DUMMY = Buf("dummy")
        TMPB = [Buf(f"tmp{i}") for i in range(NTMP)]
        SMB = [Buf(f"sm{i}") for i in range(6)]
        DGB = [Buf(f"dg{i}") for i in range(3)]
        BANK = [Buf(f"bank{i}") for i in range(8)]

        st = {"tmp": 0, "ps": 0, "issue": 0, "cons": 0, "rel": 0, "sm": 0, "dg": 0, "dc": 0}
        total_chunks = L * ntiles * NCHUNK

        def TMP():
            i = st["tmp"]
            st["tmp"] = (i + 1) % NTMP
            return tmps[i][:, :], TMPB[i]

        def PS():
            i = st["ps"]
            st["ps"] = (i + 1) % 8
            return banks[i], BANK[i]

        def chunk_src(g):
            l = g // (ntiles * NCHUNK)
            ci = g % NCHUNK
            n = CH
            if ci >= 35 and DSCHED[ci - 35][1] == 2:
                n = 6 * 256
            return wst[l * NCHUNK + ci, :, 0:n], n

        def issue(g):
            s = g % NSLOT
            src, n = chunk_src(g)
            dst = ring[s][:, 0:n]
            P.dma(f"ring{s}", lambda e, dst=dst, src=src: e.dma_start(out=dst, in_=src), writes=[RING[s]])

        def pump():
            while st["issue"] < min(total_chunks, st["rel"] + NSLOT):
                issue(st["issue"])
                st["issue"] += 1

        def consume():
            g = st["cons"]
            pump()
            assert g < st["issue"], "weight ring too shallow for this consumption pattern"
            st["cons"] = g + 1
            s = g % NSLOT
            return hv[s], RING[s]

        def done(n=1):
            st["rel"] += n
            assert st["rel"] <= st["cons"]
            pump()

        P.dma_batch("par", [
            (lambda e: e.dma_start(out=cst[:, :], in_=cst_d), (), [CONST]),
            (lambda e: e.dma_start(out=self_, in_=sel_d), (), SQ),
            (lambda e: e.dma_start(out=gpre[:, :, :], in_=pp_d[:, :, 0:8].rearrange("l p c -> p l c")), (), [CONST]),
        ])
        for t in range(ntiles):
            P.dma(f"x{t}", lambda e, t=t: e.dma_start(out=x[:, :, t * N:(t + 1) * N], in_=x_in[:, :, t * N:(t + 1) * N]),
                  writes=X[t])
        P.op("pool", lambda e: e.memset(ones_bf[:, :], 1.0), writes=[CONST])
        P.op("pool", lambda e: e.memset(neghalf[:, :], -0.5), writes=[CONST])
        P.op("pool", lambda e: e.memset(vTz[:, :, :], 0.0), writes=VT)
        P.op("dve", lambda e: e.tensor_copy(out=sel[:, :], in_=self_), reads=SQ + [CONST], writes=[CONST])

        def SM():
            i = st["sm"]
            st["sm"] = (i + 1) % 6
            return smalls[i], SMB[i]

        def bcast_prep(v3, vb):
            i = st["dg"]
            st["dg"] = (i + 1) % 3
            dg, hl, dgb = dgs[i], hls[i], DGB[i]
            P.op("dve", lambda e: e.tensor_copy(out=hl[:, 0, 0:3], in_=v3), reads=[vb], writes=[dgb])
            P.op("dve", lambda e: e.tensor_tensor(out=hl[:, 1, 0:3], in0=v3, in1=hl[:, 0, 0:3], op=ALU.subtract),
                 reads=[vb, dgb], writes=[dgb])
            P.op("dve", lambda e: e.tensor_tensor(
                out=dg[:, :, :, :], in0=ident.unsqueeze(1).unsqueeze(1).broadcast_to([128, 2, 3, 128]),
                in1=hl[:, :, 0:3].unsqueeze(3).broadcast_to([128, 2, 3, 128]), op=ALU.mult),
                reads=[dgb, CONST], writes=[dgb])
            return dg, dgb

        def bcast_pe(handle, bank=None):
            dg, dgb = handle
            ps2, p2b = bank if bank is not None else PS()
            fns = []
            for c in range(3):
                fns.append(_mm(ps2[:, c * 128:(c + 1) * 128], ones_bf[:, :], dg[:, 0, c, :], True, False))
                fns.append(_mm(ps2[:, c * 128:(c + 1) * 128], ones_bf[:, :], dg[:, 1, c, :], False, True))
            P.group("pe", fns, reads=[dgb, CONST], writes=[p2b])
            return ps2[:, 0:N], p2b

        def bcast(v3, vb):
            return bcast_pe(bcast_prep(v3, vb))

        def stat_finish(ps, pb, n):
            s3, s3b = SM()
            P.op("dve", lambda e: e.tensor_scalar(out=s3[:, 0:3], in0=ps[:, 0:3], scalar1=1.0 / n, scalar2=EPS,
                                                  op0=ALU.mult, op1=ALU.add), reads=[pb], writes=[s3b])
            P.op("pool", lambda e: e.tensor_tensor(out=s3[:, 0:3], in0=s3[:, 0:3], in1=neghalf[:, 0:3], op=ALU.pow),
                 reads=[s3b, CONST], writes=[s3b])
            return bcast_prep(s3[:, 0:3], s3b)

        def stat_prep(srcs, src_bufs, n):
            ps, pb = PS()
            fns = []
            for c in range(3):
                for i, s_ in enumerate(srcs):
                    fns.append(_mm(ps[:, c:c + 1], s_[:, c * 128:(c + 1) * 128], ones_bf[:, 0:1], i == 0, i == len(srcs) - 1))
            P.group("pe", fns, reads=list(src_bufs) + [CONST], writes=[pb])
            return stat_finish(ps, pb, n)

        def stat_rs(srcs, src_bufs, n):
            return bcast_pe(stat_prep(srcs, src_bufs, n))

        def proj(hvv, rb, blk, rhs3, rhs_buf, n, pipelined=False):
            ps, pb = PS()
            fns = [_mm(ps[:, 0:n], hvv[:, kt * 256 + blk * 128: kt * 256 + blk * 128 + 128], rhs3[:, kt, 0:n],
                       kt == 0, kt == KT - 1) for kt in range(KT)]
            rbufs = list(rhs_buf) if isinstance(rhs_buf, list) else [rhs_buf]
            if isinstance(rhs_buf, list) and pipelined:
                for kt in range(KT):
                    P.op("pe", fns[kt], reads=[rb, rhs_buf[kt]], writes=[pb])
            else:
                P.group("pe", fns, reads=[rb] + rbufs, writes=[pb])
            return ps, pb

        def premix_stages(li_, tt_):
            xt_ = x[:, :, tt_ * N:(tt_ + 1) * N]
            xb_ = X[tt_]
            box = {}

            def s1():
                P.op("act", lambda e: e.activation(out=sq[:, :, :], in_=xt_, func=AF.Square), reads=xb_, writes=SQ)

            def s2():
                box["h"] = stat_prep([sq[:, kt, :] for kt in range(KT)], SQ, 1024.0)

            def s3():
                t, tb = bcast_pe(box["h"])
                P.op("act", lambda e: e.activation(out=rs_sb[:, :], in_=t, func=AF.Identity), reads=[tb], writes=[RSSB])
                box["rs"] = (rs_sb[:, :], RSSB)

            def s4(kts=range(KT)):
                t, tb = box["rs"]
                for kt in kts:
                    P.op("dve", lambda e, kt=kt: e.scalar_tensor_tensor(
                        out=h[:, kt, :], in0=xt_[:, kt, :], scalar=gpre[:, li_, kt:kt + 1], in1=t,
                        op0=ALU.mult, op1=ALU.mult), reads=[xb_[kt], tb, CONST], writes=[H])
            return [s1, s2, s3, s4]

        def resid_then_norm(xt, xb, col0):
            t, tb = stat_rs([sq[:, m, :] for m in range(KT)], SQ, 1024.0)
            ps, pb = PS()
            for m in range(KT):
                P.op("dve", lambda e, m=m: e.tensor_tensor(out=y[:, m, :], in0=y[:, m, :], in1=t, op=ALU.mult),
                     reads=[Y[m], tb], writes=[Y[m]])
                P.op("pool", lambda e, m=m: e.tensor_tensor(out=xt[:, m, :], in0=xt[:, m, :], in1=y[:, m, :], op=ALU.add),
                     reads=[Y[m], xb[m]], writes=[xb[m]])
                P.op("act", lambda e, m=m: e.activation(out=sq[:, m, :], in_=xt[:, m, :], func=AF.Square),
                     reads=[xb[m]], writes=[SQ[m]])
                P.group("pe", [_mm(ps[:, c:c + 1], sq[:, m, c * 128:(c + 1) * 128], ones_bf[:, 0:1],
                                   (m == 0 and c == 0), m == KT - 1) for c in range(3)],
                        reads=[SQ[m], CONST], writes=[pb])
            t2, t2b = bcast_pe(stat_finish(ps, pb, 1024.0))
            for kt in range(KT):
                P.op("dve", lambda e, kt=kt: e.scalar_tensor_tensor(
                    out=h2[:, kt, 2:2 + N], in0=xt[:, kt, :], scalar=gs[:, col0 + kt:col0 + kt + 1], in1=t2,
                    op0=ALU.mult, op1=ALU.mult), reads=[xb[kt], t2b, GSB], writes=[H2[kt]])

        def postffn_tail(xt, xb, tt, last_layer, do_stash):
            handle = stat_prep([sq[:, m, :] for m in range(KT)], SQ, 1024.0)

            def fin():
                t, tb = bcast_pe(handle)
                for m in range(KT):
                    P.op("dve", lambda e, m=m: e.tensor_tensor(out=y[:, m, :], in0=y[:, m, :], in1=t, op=ALU.mult),
                         reads=[Y[m], tb], writes=[Y[m]])
                    P.op("pool", lambda e, m=m: e.tensor_tensor(out=xt[:, m, :], in0=xt[:, m, :], in1=y[:, m, :], op=ALU.add),
                         reads=[Y[m], xb[m]], writes=[xb[m]])
                if do_stash:
                    P.op("pool", lambda e: e.tensor_copy(out=h2[:, :, 0:2], in_=h2[:, :, N:N + 2]), reads=H2, writes=H2)
                if last_layer:
                    P.dma("out", lambda e: e.dma_start(out=y_out[:, :, tt * N:(tt + 1) * N],
                                                       in_=x[:, :, tt * N:(tt + 1) * N]), reads=xb)
            return fin

        def guard_switch():
            P.op("pool", lambda e: e.memset(dummy[:, :], 0.0), writes=[GUARD, DUMMY])

        units = [(a, b) for a in range(L) for b in range(ntiles)]
        for f_ in premix_stages(0, 0):
            f_()
        pend = {"tail": None}
        for li in range(L):
            P.dma_batch("par", [
                (lambda e, li=li: e.dma_start(out=pp[:, :], in_=pp_d[li]), (), [PPB]),
                (lambda e, li=li: e.dma_start(out=lng[:, :], in_=lng_d[li]), (), [LNB]),
                (lambda e, li=li: e.dma_start(out=lnb[:, :], in_=lnb_d[li]), (), [LNB]),
                (lambda e, li=li: e.dma_start(out=wTf, in_=wT_d[li].rearrange("p (h t) -> p h t", h=6)), (), SQ),
                (lambda e, li=li: e.dma_start(out=bsf, in_=bs_d[li].rearrange("p (h t) -> p h t", h=3)), (), SQ),
            ])
            for (a, b_, sc) in ((0, 40, 1.0),):
                P.op("dve", lambda e, a=a, b_=b_, sc=sc: e.tensor_scalar(
                    out=gs[:, a:b_], in0=pp[:, a:b_], scalar1=sc, scalar2=None, op0=ALU.mult),
                    reads=[PPB], writes=[GSB])
            P.op("pool", lambda e: e.tensor_tensor(
                out=diagA[:, :, :], in0=ident.unsqueeze(1).broadcast_to([128, 6, 128]),
                in1=pp[:, C_CA:C_CA + 6].unsqueeze(2).broadcast_to([128, 6, 128]), op=ALU.mult),
                reads=[PPB, CONST], writes=[DIAGA])
            P.op("pool", lambda e: e.tensor_tensor(
                out=wT[:, :, :], in0=wTf, in1=maskT.unsqueeze(1).broadcast_to([128, 6, 128]), op=ALU.mult),
                reads=SQ + [CONST], writes=[SGUW])
            P.op("dve", lambda e: e.tensor_copy(out=bsr[:, :, :], in_=bsf), reads=SQ, writes=[SGUW])
            if li == 0:
                for j_ in range(3):
                    P.op("pool", lambda e, j_=j_: e.tensor_tensor(
                        out=diagC[:, j_, :, :], in0=ident.unsqueeze(1).broadcast_to([128, 31, 128]),
                        in1=pp[:, C_CC + 31 * j_:C_CC + 31 * (j_ + 1)].unsqueeze(2).broadcast_to([128, 31, 128]),
                        op=ALU.mult), reads=[PPB, CONST], writes=[DIAGC[j_]])
            P.op("pool", lambda e: e.memset(pbuf[:, :, 0:2], 0.0), writes=PB)
            P.op("pool", lambda e: e.memset(qbuf[:, :, 0:30], 0.0), writes=QB)
            P.op("pool", lambda e: e.memset(h2[:, :, 0:2], 0.0), writes=H2)

            for tt in range(ntiles):
                xt = x[:, :, tt * N:(tt + 1) * N]
                xb = X[tt]
                ui = li * ntiles + tt
                nxt = premix_stages(*units[ui + 1]) if ui + 1 < len(units) else None
                guard_switch()

                def c_conv(j):
                    sl = j
                    ps, pb = PS()
                    P.group("pe", [_mm(ps[:, 0:N], diagC[:, sl, k, :], qbuf[:, j, k:k + N], k == 0, k == 30)
                                   for k in range(31)], reads=[QB[j], DIAGC[sl]], writes=[pb])
                    P.op("act", lambda e: e.activation(
                        out=ycp[:, j, :], in_=ps[:, 0:N], func=AF.Identity, bias=pp[:, C_CB + j:C_CB + j + 1]),
                        reads=[pb, PPB, GUARD], writes=[YCP])
                    P.op("act", lambda e: e.activation(
                        out=sqc[:, j, :], in_=ps[:, 0:N], func=AF.Square, bias=pp[:, C_CB + j:C_CB + j + 1]),
                        reads=[pb, PPB, GUARD], writes=[SQC[j]])
                    P.op("pool", lambda e: e.tensor_copy(out=ycb[:, j, :], in_=ycp[:, j, :]),
                         reads=[YCP, GUARD], writes=[YCB[j]])
                    P.op("pool", lambda e: e.tensor_copy(out=qbuf[:, j, 0:30], in_=qbuf[:, j, N:N + 30]),
                         reads=[QB[j]], writes=[QB[j]])

                for j in range(3):
                    cg, cgb = consume()
                    ps, pb = proj(cg, cgb, 0, h, H, N)
                    sg, sgb = TMP()
                    P.op("act", lambda e, ps=ps, sg=sg: e.activation(out=sg, in_=ps[:, 0:N], func=AF.Sigmoid),
                         reads=[pb], writes=[sgb])
                    ps, pb = proj(cg, cgb, 1, h, H, N)
                    done(1)
                    P.op("dve", lambda e, j=j, ps=ps, sg=sg: e.tensor_tensor(
                        out=qbuf[:, j, 30:30 + N], in0=sg, in1=ps[:, 0:N], op=ALU.mult),
                        reads=[pb, sgb], writes=[QB[j]])
                if pend["tail"] is not None:
                    pend["tail"]()
                    pend["tail"] = None
                c_conv(0)
                c0, c0b = consume()
                c1, c1b = consume()
                c2, c2b = consume()
                psv = [PS() for _ in range(3)]
                for c in range(3):
                    ps, pb = psv[c]
                    P.group("pe", [_mm(ps[:, 0:256], h[:, kt, c * 128:(c + 1) * 128], c0[:, kt * 256:(kt + 1) * 256],
                                       kt == 0, kt == KT - 1) for kt in range(KT)], reads=[H, c0b], writes=[pb])
                for c in range(3):
                    ps, pb = psv[c]
                    P.group("pe", [_mm(ps[:, 256:384], h[:, kt, c * 128:(c + 1) * 128], c1[:, kt * 256:kt * 256 + 128],
                                       kt == 0, kt == KT - 1) for kt in range(KT)], reads=[H, c1b], writes=[pb])
                for c in range(3):
                    ps, pb = psv[c]
                    gv, gvb = TMP()
                    P.op("act", lambda e, ps=ps, gv=gv: e.activation(out=gv, in_=ps[:, 0:384], func=AF.Gelu),
                         reads=[pb], writes=[gvb])
                    P.op("dve", lambda e, c=c, gv=gv: e.bn_stats(out=bnst[:, c, :], in_=gv), reads=[gvb], writes=[BN[c]])
                    P.op("dve", lambda e, c=c: e.bn_aggr(out=mv[:, c, :], in_=bnst[:, c, :]), reads=[BN[c]], writes=[MV[c]])
                    P.op("dve", lambda e, c=c: e.tensor_scalar(out=rv[:, c, :], in0=mv[:, c, 1:2], scalar1=EPS,
                                                               scalar2=None, op0=ALU.add), reads=[MV[c]], writes=[RV[c]])
                    P.op("pool", lambda e, c=c: e.tensor_tensor(out=rv[:, c, :], in0=rv[:, c, :], in1=neghalf[:, 0:1],
                                                                op=ALU.pow), reads=[RV[c], CONST], writes=[RV[c]])
                    vn, vnb = TMP()
                    P.op("dve", lambda e, c=c, gv=gv, vn=vn: e.tensor_scalar(
                        out=vn, in0=gv, scalar1=mv[:, c, 0:1], scalar2=rv[:, c, :], op0=ALU.subtract, op1=ALU.mult),
                        reads=[gvb, MV[c], RV[c]], writes=[vnb])
                    P.op("pool", lambda e, vn=vn: e.tensor_tensor(out=vn, in0=vn, in1=lng[:, :], op=ALU.mult),
                         reads=[vnb, LNB], writes=[vnb])
                    base = vTz[:, c, :]
                    dst = bass.AP(base.tensor, base.offset, [list(base.ap[0]), [256, 3], [192, 2], [1, 64]])
                    P.op("pool", lambda e, vn=vn, dst=dst: e.tensor_tensor(
                        out=dst, in0=vn.rearrange("p (j h d) -> p j h d", j=3, h=2),
                        in1=lnb[:, :].rearrange("p (j h d) -> p j h d", j=3, h=2), op=ALU.add),
                        reads=[vnb, LNB], writes=[VT[c]])
                for j, (cv, cb_, blk) in enumerate(((c1, c1b, 1), (c2, c2b, 0), (c2, c2b, 1))):
                    ps, pb = proj(cv, cb_, blk, h, H, N)
                    P.op("act", lambda e, j=j, ps=ps: e.activation(out=u[:, j, :], in_=ps[:, 0:N], func=AF.Gelu),
                         reads=[pb, GUARD], writes=[U[j]])
                done(3)
                c_conv(1)
                c3, c3b = consume()
                c4, c4b = consume()
                c5, c5b = consume()
                a_src = (((c3, c3b, 0), (c3, c3b, 1), (c4, c4b, 0)), ((c4, c4b, 1), (c5, c5b, 0), (c5, c5b, 1)))
                def a_block(j):
                    (cc, ccb, cblk), (cx, cxb, xblk), (cbg, cbgb, bblk) = a_src[j]
                    ps, pb = proj(cc, ccb, cblk, h, H, N)
                    csb, csbb = TMP()
                    P.op("act", lambda e, ps=ps, csb=csb: e.activation(out=csb, in_=ps[:, 0:N], func=AF.Identity),
                         reads=[pb], writes=[csbb])
                    ps, pb = proj(cx, cxb, xblk, h, H, N)
                    P.op("dve", lambda e, j=j, ps=ps, csb=csb: e.tensor_tensor(
                        out=pbuf[:, j, 2:2 + N], in0=csb, in1=ps[:, 0:N], op=ALU.mult),
                        reads=[pb, csbb], writes=[PB[j]])
                    ps, pb = proj(cbg, cbgb, bblk, h, H, N)
                    bsb, bsbb = TMP()
                    P.op("act", lambda e, ps=ps, bsb=bsb: e.activation(out=bsb, in_=ps[:, 0:N], func=AF.Identity),
                         reads=[pb], writes=[bsbb])
                    ps, pb = PS()
                    P.group("pe", [_mm(ps[:, 0:N], diagA[:, j * 3 + k, :], pbuf[:, j, k:k + N], k == 0, k == 2)
                                   for k in range(3)], reads=[PB[j], DIAGA], writes=[pb])
                    P.op("dve", lambda e, j=j, ps=ps, bsb=bsb: e.tensor_tensor(
                        out=y[:, j, :], in0=bsb, in1=ps[:, 0:N], op=ALU.mult), reads=[pb, bsbb], writes=[Y[j]])
                    P.op("pool", lambda e, j=j: e.tensor_copy(out=pbuf[:, j, 0:2], in_=pbuf[:, j, N:N + 2]),
                         reads=[PB[j]], writes=[PB[j]])
                a_block(0)
                a_block(1)
                done(3)
                c_conv(2)
                psl, plb = PS()
                fns = []
                for c in range(3):
                    for j in range(3):
                        fns.append(_mm(psl[:, c:c + 1], ycb[:, j, c * 128:(c + 1) * 128], ones_bf[:, 0:1], j == 0, j == 2))
                for c in range(3):
                    for j in range(3):
                        fns.append(_mm(psl[:, 3 + c:4 + c], sqc[:, j, c * 128:(c + 1) * 128], ones_bf[:, 0:1], j == 0, j == 2))
                P.group("pe", fns, reads=YCB + SQC + [CONST, GUARD], writes=[plb])
                m3, m3b = SM()
                w3, w3b = SM()
                P.op("dve", lambda e, m3=m3, psl=psl: e.tensor_scalar(
                    out=m3[:, 0:6], in0=psl[:, 0:6], scalar1=1.0 / 384.0, scalar2=None, op0=ALU.mult),
                    reads=[plb], writes=[m3b])
                P.op("dve", lambda e, m3=m3, w3=w3: e.tensor_tensor(out=w3[:, 0:3], in0=m3[:, 0:3], in1=m3[:, 0:3], op=ALU.mult),
                     reads=[m3b], writes=[w3b])
                P.op("dve", lambda e, m3=m3, w3=w3: e.scalar_tensor_tensor(
                    out=m3[:, 3:6], in0=m3[:, 3:6], scalar=EPS, in1=w3[:, 0:3], op0=ALU.add, op1=ALU.subtract),
                    reads=[m3b, w3b], writes=[m3b])
                P.op("pool", lambda e, m3=m3: e.tensor_tensor(out=m3[:, 3:6], in0=m3[:, 3:6], in1=neghalf[:, 0:3], op=ALU.pow),
                     reads=[m3b, CONST], writes=[m3b])
                P.op("dve", lambda e, m3=m3, w3=w3: e.tensor_tensor(out=w3[:, 3:6], in0=m3[:, 0:3], in1=m3[:, 3:6], op=ALU.mult),
                     reads=[m3b, w3b], writes=[w3b])
                hA = bcast_prep(m3[:, 3:6], m3b)
                hB = bcast_prep(w3[:, 3:6], w3b)
                for j in range(3):
                    ps, pb = PS()
                    fns = []
                    for c in range(3):
                        o_ = ps[:, c * 128:(c + 1) * 128]
                        fns.append(_mm(o_, vTz[:, c, (2 * j) * 128:(2 * j + 1) * 128], wT[:, 2 * j, :], True, False))
                        fns.append(_mm(o_, vTz[:, c, (2 * j + 1) * 128:(2 * j + 2) * 128], wT[:, 2 * j + 1, :], False, False))
                        fns.append(_mm(o_, sel[0:2, :], bsr[0:2, j, :], False, True))
                    P.group("pe", fns, reads=VT + [SGUW, CONST], writes=[pb])
                    P.op("dve", lambda e, j=j, ps=ps: e.tensor_tensor(
                        out=y[:, 2 + j, :], in0=u[:, j, :], in1=ps[:, 0:N], op=ALU.mult),
                        reads=[pb, U[j], GUARD], writes=[Y[2 + j]])
                rA, rAb = bcast_pe(hA, bank=(psl, plb))
                gh = {}
                for (g_, b0, b1, n_) in (("A", 0, 2, 256.0), ("B", 2, 5, 384.0)):
                    P.op("act", lambda e, b0=b0, b1=b1: e.activation(out=sq[:, b0:b1, :], in_=y[:, b0:b1, :], func=AF.Square),
                         reads=Y[b0:b1], writes=SQ[b0:b1])
                rB, rBb = bcast_pe(hB)
                P.op("dve", lambda e, rA=rA: e.tensor_tensor(
                    out=ycp, in0=ycp, in1=rA.unsqueeze(1).broadcast_to([128, 3, N]), op=ALU.mult),
                    reads=[YCP, rAb, GUARD], writes=[YCP])
                P.op("dve", lambda e, rB=rB: e.tensor_tensor(
                    out=ycp, in0=ycp, in1=rB.unsqueeze(1).broadcast_to([128, 3, N]), op=ALU.subtract),
                    reads=[YCP, rBb, GUARD], writes=[YCP])
                for (g_, b0, b1, n_) in (("A", 0, 2, 256.0), ("B", 2, 5, 384.0)):
                    gh[g_] = stat_prep([sq[:, i, :] for i in range(b0, b1)], SQ[b0:b1], n_)
                for j in range(3):
                    P.op("act", lambda e, j=j: e.activation(
                        out=y[:, 5 + j, :], in_=ycp[:, j, :], func=AF.Silu,
                        scale=pp[:, C_LG + j:C_LG + j + 1], bias=pp[:, C_LB + j:C_LB + j + 1]),
                        reads=[YCP, PPB, GUARD], writes=[Y[5 + j]])
                def ynorm_ops(t, tb, b0, b1):
                    for i in range(b0, b1):
                        P.op("dve", lambda e, i=i: e.scalar_tensor_tensor(
                            out=h[:, i, :], in0=y[:, i, :], scalar=gs[:, C_GRP + i:C_GRP + i + 1], in1=t,
                            op0=ALU.mult, op1=ALU.mult), reads=[Y[i], tb, GSB], writes=[H])

                t, tb = bcast_pe(gh["A"])
                ynorm_ops(t, tb, 0, 2)
                t, tb = bcast_pe(gh["B"])
                ynorm_ops(t, tb, 2, 5)
                P.op("act", lambda e: e.activation(out=sq[:, 5:8, :], in_=y[:, 5:8, :], func=AF.Square),
                     reads=Y[5:8], writes=SQ[5:8])
                hC = stat_prep([sq[:, i, :] for i in range(5, 8)], SQ[5:8], 384.0)
                early = []
                for mp in range(2):
                    cw, cwb = consume()
                    for half in range(2):
                        ps, pb = PS()
                        P.group("pe", [_mm(ps[:, 0:N], cw[:, kt * 256 + half * 128: kt * 256 + half * 128 + 128], h[:, kt, :],
                                           kt == 0, False) for kt in range(5)], reads=[cwb, H], writes=[pb])
                        early.append((ps, pb, cw, cwb, half))
                t, tb = bcast_pe(hC)
                ynorm_ops(t, tb, 5, 8)

                def out_evac(m, ps, pb):
                    P.op("act", lambda e: e.activation(out=sq[:, m, :], in_=ps[:, 0:N], func=AF.Square),
                         reads=[pb], writes=[SQ[m]])
                    P.op("act", lambda e: e.activation(
                        out=y[:, m, :], in_=ps[:, 0:N], func=AF.Identity, scale=gs[:, C_POST + m:C_POST + m + 1]),
                        reads=[pb, GSB], writes=[Y[m]])

                for m, (ps, pb, cw, cwb, half) in enumerate(early):
                    P.group("pe", [_mm(ps[:, 0:N], cw[:, kt * 256 + half * 128: kt * 256 + half * 128 + 128], h[:, kt, :],
                                       False, kt == KT - 1) for kt in range(5, KT)], reads=[cwb, H], writes=[pb])
                    out_evac(m, ps, pb)
                done(2)
                for mp in range(2, 4):
                    cw, cwb = consume()
                    for half in range(2):
                        m = 2 * mp + half
                        ps, pb = proj(cw, cwb, half, h, H, N)
                        out_evac(m, ps, pb)
                    done(1)
                resid_then_norm(xt, xb, C_PFF)
                guard_switch()
                for j in range(NFF):
                    if nxt is not None:
                        if j in (2, 5, 9):
                            nxt[(2, 5, 9).index(j)]()
                        elif 11 <= j < 19:
                            nxt[3]([j - 11])
                    cw, cwb = consume()
                    psg, pgb = proj(cw, cwb, 0, h2, H2, N + 2, pipelined=(j == 0))
                    psv_, pvb = proj(cw, cwb, 1, h2, H2, N + 2)
                    done(1)
                    tg, tgb = TMP()
                    tv, tvb = TMP()
                    cg0 = C_FG + 3 * j
                    cv0 = C_FV + 3 * j
                    P.op("act", lambda e, tg=tg, psg=psg, cg0=cg0: e.activation(
                        out=tg, in_=psg[:, 0:N], func=AF.Identity, scale=pp[:, cg0:cg0 + 1]),
                        reads=[pgb, PPB], writes=[tgb])
                    P.op("act", lambda e, tv=tv, psv_=psv_, cv0=cv0: e.activation(
                        out=tv, in_=psv_[:, 0:N], func=AF.Identity, scale=pp[:, cv0:cv0 + 1]),
                        reads=[pvb, PPB], writes=[tvb])
                    for k in (1, 2):
                        P.op("dve", lambda e, tg=tg, psg=psg, cg0=cg0, k=k: e.scalar_tensor_tensor(
                            out=tg, in0=psg[:, k:k + N], scalar=pp[:, cg0 + k:cg0 + k + 1], in1=tg,
                            op0=ALU.mult, op1=ALU.add), reads=[pgb, tgb, PPB], writes=[tgb])
                        P.op("dve", lambda e, tv=tv, psv_=psv_, cv0=cv0, k=k: e.scalar_tensor_tensor(
                            out=tv, in0=psv_[:, k:k + N], scalar=pp[:, cv0 + k:cv0 + k + 1], in1=tv,
                            op0=ALU.mult, op1=ALU.add), reads=[pvb, tvb, PPB], writes=[tvb])
                    sgt, sgtb = TMP()
                    P.op("act", lambda e, tg=tg, sgt=sgt: e.activation(out=sgt, in_=tg, func=AF.Silu),
                         reads=[tgb], writes=[sgtb])
                    P.op("pool", lambda e, j=j, sgt=sgt, tv=tv: e.tensor_tensor(
                        out=gated[:, j, :], in0=sgt, in1=tv, op=ALU.mult),
                        reads=[sgtb, tvb, GUARD], writes=[G[j]])
                if tt == ntiles - 1 and li + 1 < L:
                    P.dma("ccw", lambda e, li=li: e.dma_start(out=ccw[:, :], in_=pp_d[li + 1][:, C_CC:C_CC + 93]), writes=[CCWB])
                    for j_ in range(3):
                        P.op("pool", lambda e, j_=j_: e.tensor_tensor(
                            out=diagC[:, j_, :, :], in0=ident.unsqueeze(1).broadcast_to([128, 31, 128]),
                            in1=ccw[:, 31 * j_:31 * (j_ + 1)].unsqueeze(2).broadcast_to([128, 31, 128]),
                            op=ALU.mult), reads=[CCWB, CONST], writes=[DIAGC[j_]])
                dbanks = {}
                for (mp, part) in DSCHED:
                    if mp not in dbanks:
                        dbanks[mp] = (PS(), PS())
                    (ps0, p0b), (ps1, p1b) = dbanks[mp]
                    cw, cwb = consume()
                    nk = 8 if part < 2 else 6
                    fns = []
                    for kk in range(nk):
                        kt = part * 8 + kk
                        fns.append(_mm(ps0[:, 0:N], cw[:, kk * 256:kk * 256 + 128], gated[:, kt, :], kt == 0, kt == NFF - 1))
                        fns.append(_mm(ps1[:, 0:N], cw[:, kk * 256 + 128:kk * 256 + 256], gated[:, kt, :], kt == 0, kt == NFF - 1))
                    P.group("pe", fns, reads=[cwb, GUARD] + G[part * 8:part * 8 + nk], writes=[p0b, p1b])
                    done(1)
                    if part == 2:
                        for half, (ps, pb) in enumerate(((ps0, p0b), (ps1, p1b))):
                            m = 2 * mp + half
                            P.op("act", lambda e, m=m, ps=ps: e.activation(out=sq[:, m, :], in_=ps[:, 0:N], func=AF.Square),
                                 reads=[pb], writes=[SQ[m]])
                            P.op("act", lambda e, m=m, ps=ps: e.activation(
                                out=y[:, m, :], in_=ps[:, 0:N], func=AF.Identity, scale=gs[:, C_OFF + m:C_OFF + m + 1]),
                                reads=[pb, GSB], writes=[Y[m]])
                pend["tail"] = postffn_tail(xt, xb, tt, li == L - 1, tt < ntiles - 1)
        pend["tail"]()
        if debug:
            allb = [H, GUARD, SGUW, GSB] + H2 + Y + G + QB + PB + VT
            P.dma_batch("out", [
                (lambda e: e.dma_start(out=dbg_d["d_h"], in_=h[:, :, :]), allb, ()),
                (lambda e: e.dma_start(out=dbg_d["d_h2"], in_=h2[:, :, :]), allb, ()),
                (lambda e: e.dma_start(out=dbg_d["d_y"], in_=y[:, :, :]), allb, ()),
                (lambda e: e.dma_start(out=dbg_d["d_R"], in_=R[:, :]), allb, ()),
                (lambda e: e.dma_start(out=dbg_d["d_q"], in_=qbuf[:, :, :]), allb, ()),
                (lambda e: e.dma_start(out=dbg_d["d_p"], in_=pbuf[:, :, :]), allb, ()),
                (lambda e: e.dma_start(out=dbg_d["d_vT"], in_=vTz[:, :, :]), allb, ()),
                (lambda e: e.dma_start(out=dbg_d["d_gs"], in_=gs[:, :]), allb, ()),
                (lambda e: e.dma_start(out=dbg_d["d_wT"], in_=wT[:, :, :]), allb, ()),
            ])
        P.ops["sp"].append(("wait", "out", P.dmacnt["out"]))

        global LAST_PELOG
        LAST_PELOG = P.pelog
        semkeys = set(P.ENGS) | set(P.dmacnt.keys())
        sems = {k: es.enter_context(nc.semaphore(f"s_{k}")) for k in sorted(semkeys)}
        block = es.enter_context(nc.Block())

        def emit(e, eng):
            mysem = sems[eng]
            for o in P.ops[eng]:
                if o[0] == "wait":
                    e.wait_ge(sems[o[1]], o[2])
                elif o[0] == "ins":
                    ins = o[1](e)
                    if o[2]:
                        ins.then_inc(mysem, 1)
                else:
                    o[1](e).then_inc(sems[o[2]], 16)

        @block.sync
        def _(e):
            emit(e, "sp")

        @block.tensor
        def _(e):
            emit(e, "pe")

        @block.scalar
        def _(e):
            emit(e, "act")

        @block.vector
        def _(e):
            emit(e, "dve")

        @block.gpsimd
        def _(e):
            emit(e, "pool")
    return nc


W_IN_BLOCKS = [1920, 1536, 2048, 1664, 2176, 1792,
               1152, 1280, 1408, 768, 896, 1024,
               256, 512, 0, 384, 640, 128]


def _fm(v):
    return np.ascontiguousarray(v.reshape(-1, 128).T)


def prep_layer_weights(inp, l):
    chunks = np.zeros((NCHUNK, 128, CH), np.float32)
    w_in = inp["w_in"][l]
    cols = np.concatenate([np.arange(c, c + 128) for c in W_IN_BLOCKS])
    wi = w_in[:, cols].reshape(KT, 128, 9, 256)
    chunks[0:9] = wi.transpose(2, 1, 0, 3).reshape(9, 128, CH)
    wo = inp["w_out"][l].reshape(KT, 128, 4, 256)
    chunks[9:13] = wo.transpose(2, 1, 0, 3).reshape(4, 128, CH)
    wu = inp["w_up"][l]
    wg = wu[:, :DFF].reshape(KT, 128, NFF, 128)
    wv = wu[:, DFF:].reshape(KT, 128, NFF, 128)
    wgv = np.concatenate([wg, wv], axis=3)
    chunks[13:35] = wgv.transpose(2, 1, 0, 3).reshape(NFF, 128, CH)
    wd = inp["w_down"][l].reshape(NFF, 128, 4, 256)
    ci = 35
    for (mp, part) in DSCHED:
        nk = 8 if part < 2 else 6
        blk = wd[part * 8:part * 8 + nk, :, mp, :]
        chunks[ci, :, 0:nk * 256] = blk.transpose(1, 0, 2).reshape(128, nk * 256)
        ci += 1
    return chunks


def prep_layer_params(inp, l):
    pp = np.zeros((128, NP), np.float32)
    pp[:, C_PRE:C_PRE + 8] = _fm(inp["pre_mix_g"][l])
    pp[:, C_GRP:C_GRP + 8] = _fm(inp["grp_norm_g"][l])
    pp[:, C_POST:C_POST + 8] = _fm(inp["post_mix_g"][l])
    pp[:, C_PFF:C_PFF + 8] = _fm(inp["pre_ffn_g"][l])
    pp[:, C_OFF:C_OFF + 8] = _fm(inp["post_ffn_g"][l])
    ca = inp["conv_a_w"][l]
    for j in range(2):
        for k in range(3):
            pp[:, C_CA + 3 * j + k] = ca[k, j * 128:(j + 1) * 128]
    cc = inp["conv_c_w"][l]
    for j in range(3):
        pp[:, C_CC + 31 * j:C_CC + 31 * (j + 1)] = cc[:, j * 128:(j + 1) * 128].T
    pp[:, C_CB:C_CB + 3] = _fm(inp["conv_c_b"][l])
    pp[:, C_LG:C_LG + 3] = _fm(inp["conv_ln_g"][l])
    pp[:, C_LB:C_LB + 3] = _fm(inp["conv_ln_b"][l])
    fw = inp["ffn_conv_w"][l]
    for j in range(NFF):
        pp[:, C_FG + 3 * j:C_FG + 3 * j + 3] = fw[:, j * 128:(j + 1) * 128].T
        pp[:, C_FV + 3 * j:C_FV + 3 * j + 3] = fw[:, DFF + j * 128:DFF + (j + 1) * 128].T
    lng = np.ascontiguousarray(np.broadcast_to(inp["sgu_ln_g"][l][None, :], (128, 384))).astype(np.float32)
    lnb = np.ascontiguousarray(np.broadcast_to(inp["sgu_ln_b"][l][None, :], (128, 384))).astype(np.float32)
    wT = np.ascontiguousarray(inp["sgu_w"][l].transpose(2, 0, 1)).reshape(128, 6 * 128)
    sb = inp["sgu_b"][l]
    bs = np.ascontiguousarray(sb.reshape(3, 2, 128).transpose(1, 0, 2)).reshape(2, 3 * 128)
    return pp, lng, lnb, wT, bs


def consts():
    cst = np.zeros((128, 256), np.float32)
    cst[:, 0:128] = np.eye(128, dtype=np.float32)
    s = np.arange(128)
    cst[:, 128:256] = (s[:, None] <= s[None, :]).astype(np.float32)
    sel = np.zeros((2, 128), np.float32)
    sel[0, 0:64] = 1.0
    sel[1, 64:128] = 1.0
    return cst, sel


def core_tokens(c):
    b, half = divmod(c, 2)
    start = 0 if half == 0 else SEQ - T
    return b, half, start


def shard_x(xfull):
    outs = []
    for c in range(8):
        b, half, start = core_tokens(c)
        xs = xfull[b, start:start + T, :]
        outs.append(np.ascontiguousarray(xs.T.reshape(KT, 128, T).transpose(1, 0, 2)))
    return outs


def unshard_y(ys, out):
    for c in range(8):
        b, half, start = core_tokens(c)
        yt = ys[c].transpose(1, 0, 2).reshape(D, T).T
        if half == 0:
            out[b, 0:SEQ // 2] = yt[0:SEQ // 2]
        else:
            out[b, SEQ // 2:] = yt[HALO:]


_PROG_CACHE = {}
LAST_PELOG = None


def run_layers(xs, inp, layers):
    L = len(layers)
    if L not in _PROG_CACHE:
        _PROG_CACHE[L] = build_program(L)
    nc = _PROG_CACHE[L]
    wst = np.concatenate([prep_layer_weights(inp, l) for l in layers], axis=0)
    prm = [prep_layer_params(inp, l) for l in layers]
    pp = np.stack([p[0] for p in prm])
    lng = np.stack([p[1] for p in prm])
    lnb = np.stack([p[2] for p in prm])
    wT = np.stack([p[3] for p in prm])
    bs = np.stack([p[4] for p in prm])
    cst, sel = consts()
    in_maps = [{"x_in": xs[c], "wst": wst, "pp": pp, "lng": lng, "lnb": lnb, "sguwT": wT, "sgub": bs,
                "cst": cst, "sel": sel} for c in range(8)]
    res = run_bass_kernel_spmd(nc, in_maps, core_ids=list(range(8)))
    return [np.asarray(r["y_out"]) for r in res.results]


FUSED = True


def kernel(**inputs):
    inp = {k: np.asarray(v) for k, v in inputs.items()}
    xs = shard_x(inp["x"].astype(np.float32, copy=False))
    if FUSED:
        ys = run_layers(xs, inp, list(range(DEPTH)))
    else:
        ys = xs
        for l in range(DEPTH):
            ys = run_layers(ys, inp, [l])
    out = np.empty((BATCH, SEQ, D), np.float32)
    unshard_y(ys, out)
    return out
```

```python
import numpy as np
from contextlib import ExitStack
import concourse.bass as bass
import concourse.mybir as mybir
from concourse.bass_utils import run_bass_kernel_spmd

F32 = mybir.dt.float32
BF16 = mybir.dt.bfloat16
AF = mybir.ActivationFunctionType
ALU = mybir.AluOpType

D = 1024
KT = 8
SEQ = 4096
BATCH = 4
DEPTH = 4
HALO = 256
T = SEQ // 2 + HALO
N = 384
NTILES = T // N
DFF = 2816
NFF = DFF // 128
EPS = 1e-6
NSLOT = 4
CH = 2048
NCHUNK = 9 + 4 + 22 + 12
NP = 280
NTMP = 6
DSCHED = [(0, 0), (0, 1), (1, 0), (1, 1), (0, 2), (1, 2), (2, 0), (2, 1), (2, 2), (3, 0), (3, 1), (3, 2)]

C_PRE, C_GRP, C_POST, C_PFF, C_OFF = 0, 8, 16, 24, 32
C_CA, C_CC, C_CB, C_LG, C_LB, C_FG, C_FV = 40, 46, 139, 142, 145, 148, 214


class Buf:
    __slots__ = ("name", "w", "r")

    def __init__(self, name):
        self.name = name
        self.w = {}
        self.r = {}


class Prog:
    ENGS = ("pe", "act", "dve", "pool", "sp")

    def __init__(self):
        self.ops = {e: [] for e in self.ENGS}
        self.cnt = {e: 0 for e in self.ENGS}
        self.waited = {e: {} for e in self.ENGS}
        self.dmacnt = {}
        self.pelog = []
        self.npe = 0

    def _deps(self, eng, reads, writes):
        deps = {}
        for b in reads:
            for k, v in b.w.items():
                if deps.get(k, 0) < v:
                    deps[k] = v
        for b in writes:
            for k, v in b.w.items():
                if k != eng and deps.get(k, 0) < v:
                    deps[k] = v
            for k, v in b.r.items():
                if k != eng and deps.get(k, 0) < v:
                    deps[k] = v
        wd = self.waited[eng]
        for k, v in deps.items():
            if wd.get(k, 0) < v:
                self.ops[eng].append(("wait", k, v))
                wd[k] = v

    @staticmethod
    def _mark(key, val, reads, writes):
        for b in reads:
            if b.r.get(key, 0) < val:
                b.r[key] = val
        for b in writes:
            if b.r:
                b.w = {key: val}
                b.r = {}
            else:
                b.w[key] = val

    def group(self, eng, fns, reads=(), writes=()):
        if eng == "pe":
            self.pelog.append((self.npe, len(fns), [b.name for b in reads], [b.name for b in writes]))
            self.npe += len(fns)
        self._deps(eng, reads, writes)
        self.cnt[eng] += 1
        c = self.cnt[eng]
        for f in fns[:-1]:
            self.ops[eng].append(("ins", f, False))
        self.ops[eng].append(("ins", fns[-1], True))
        self._mark(eng, c, reads, writes)

    def op(self, eng, fn, reads=(), writes=()):
        self.group(eng, [fn], reads, writes)

    def dma_batch(self, semkey, items, q="sp"):
        for fn, reads, writes in items:
            self._deps(q, reads, writes)
        total = self.dmacnt.get(semkey, 0) + 16 * len(items)
        self.dmacnt[semkey] = total
        for fn, reads, writes in items:
            self.ops[q].append(("dma", fn, semkey))
            self._mark(semkey, total, reads, writes)

    def dma(self, semkey, fn, reads=(), writes=(), q="sp"):
        self.dma_batch(semkey, [(fn, reads, writes)], q=q)


def _mm(out, lhsT, rhs, start, stop):
    return lambda e: e.matmul(out, lhsT=lhsT, rhs=rhs, start=start, stop=stop)


def build_program(L, layer0=0, ntiles=NTILES, debug=False):
    nc = bass.Bass("TRN2", target_bir_lowering=False)
    TT = ntiles * N
    x_in = nc.dram_tensor("x_in", [128, KT, TT], F32, kind="ExternalInput").ap()
    wst = nc.dram_tensor("wst", [L * NCHUNK, 128, CH], F32, kind="ExternalInput").ap()
    pp_d = nc.dram_tensor("pp", [L, 128, NP], F32, kind="ExternalInput").ap()
    lng_d = nc.dram_tensor("lng", [L, 128, 384], F32, kind="ExternalInput").ap()
    lnb_d = nc.dram_tensor("lnb", [L, 128, 384], F32, kind="ExternalInput").ap()
    wT_d = nc.dram_tensor("sguwT", [L, 128, 6 * 128], F32, kind="ExternalInput").ap()
    bs_d = nc.dram_tensor("sgub", [L, 2, 3 * 128], F32, kind="ExternalInput").ap()
    cst_d = nc.dram_tensor("cst", [128, 256], F32, kind="ExternalInput").ap()
    sel_d = nc.dram_tensor("sel", [2, 128], F32, kind="ExternalInput").ap()
    y_out = nc.dram_tensor("y_out", [128, KT, TT], F32, kind="ExternalOutput").ap()

    if debug:
        dbg = {"d_h": ([128, KT, N], BF16), "d_h2": ([128, KT, N + 2], BF16), "d_y": ([128, KT, N], F32),
               "d_R": ([128, NFF * N], BF16), "d_q": ([128, 3, N + 30], BF16), "d_p": ([128, 2, N + 2], BF16),
               "d_x1": ([128, KT, N], F32), "d_h1": ([128, KT, N], BF16), "d_sq1": ([128, KT, N], BF16), "d_t1": ([128, N], F32),
               "d_vT": ([128, 3, 768], BF16), "d_gs": ([128, 40], F32), "d_wT": ([128, 6, 128], BF16)}
        dbg_d = {k: nc.dram_tensor(k, sh, dt, kind="ExternalOutput").ap() for k, (sh, dt) in dbg.items()}
    P = Prog()
    es = ExitStack()

    def sb(name, shape, dt):
        return es.enter_context(nc.sbuf_tensor("sb_" + name, shape, dt))

    with es:
        x = sb("x", [128, KT, TT], F32)
        ring = [sb(f"ring{i}", [128, CH], F32) for i in range(NSLOT)]
        ones_bf = sb("ones_bf", [128, 128], BF16)
        cst = sb("cst", [128, 256], F32)
        neghalf = sb("neghalf", [128, 8], F32)
        smalls = [sb(f"sm{i}", [128, 8], F32) for i in range(6)]
        dgs = [sb(f"dg{i}", [128, 2, 3, 128], BF16) for i in range(3)]
        hls = [sb(f"hl{i}", [128, 2, 4], BF16) for i in range(3)]
        pp = sb("ppt", [128, NP], F32)
        gs = sb("gs", [128, 40], F32)
        gpre = sb("gpre", [128, L, 8], F32)
        rs_sb = sb("rs_sb", [128, N], F32)
        ccw = sb("ccw", [128, 93], F32)
        diagA = sb("diagA", [128, 6, 128], BF16)
        diagC = sb("diagC", [128, 3, 31, 128], BF16)
        wT = sb("wT", [128, 6, 128], BF16)
        bsr = sb("bsr", [2, 3, 128], BF16)
        sel = sb("sel", [2, 128], BF16)
        lng = sb("lng", [128, 384], F32)
        lnb = sb("lnb", [128, 384], F32)
        h = sb("h", [128, KT, N], BF16)
        sq = sb("sq", [128, KT, N], BF16)
        y = sb("y", [128, KT, N], F32)
        h2 = sb("h2", [128, KT, N + 2], BF16)
        R = sb("R", [128, NFF * N], BF16)
        pbuf = sb("pbuf", [128, 2, N + 2], BF16)
        qbuf = sb("qbuf", [128, 3, N + 30], BF16)
        vTz = sb("vTz", [128, 3, 768], BF16)
        bnst = sb("bnst", [128, 3, 6], F32)
        mv = sb("mv", [128, 3, 2], F32)
        rv = sb("rv", [128, 3, 1], F32)
        dummy = sb("dummy", [128, 2], F32)
        tmps = [sb(f"tmp{i}", [128, N], F32) for i in range(NTMP)]
        banks = [es.enter_context(nc.psum_tensor(f"ps{i}", [128, 512], F32)) for i in range(8)]

        sqflat = sq[:, :, :].rearrange("p k n -> p (k n)")
        wTf = sqflat[:, 0:1536].bitcast(F32).rearrange("p (h t) -> p h t", h=6)
        bsf = sqflat[0:2, 1536:2304].bitcast(F32).rearrange("p (h t) -> p h t", h=3)
        self_ = sqflat[0:2, 2304:2560].bitcast(F32)
        gated = R[:, :].rearrange("p (j n) -> p j n", j=NFF)
        u = R[:, 0:2304].bitcast(F32).rearrange("p (j n) -> p j n", j=3)
        ycp = R[:, 2304:4608].bitcast(F32).rearrange("p (j n) -> p j n", j=3)
        ycb = R[:, 4608:5760].rearrange("p (j n) -> p j n", j=3)
        sqc = R[:, 5760:6912].rearrange("p (j n) -> p j n", j=3)
        ident = cst[:, 0:128]
        maskT = cst[:, 128:256]
        hv = [r[:, :].bitcast(BF16)[:, 1::2] for r in ring]

        X = [[Buf(f"x{t}_{k}") for k in range(KT)] for t in range(ntiles)]
        RING = [Buf(f"ring{i}") for i in range(NSLOT)]
        CONST = Buf("const")
        PPB, GSB, DIAGA, SGUW, LNB = Buf("pp"), Buf("gs"), Buf("diagA"), Buf("sguw"), Buf("ln")
        DIAGC = [Buf("diagC0"), Buf("diagC1"), Buf("diagC2")]
        WTF, BSF = Buf("wTf"), Buf("bsf")
        H, H2 = Buf("h"), [Buf(f"h2_{k}") for k in range(KT)]
        SQ = [Buf(f"sq{i}") for i in range(KT)]
        Y = [Buf(f"y{i}") for i in range(KT)]
        G = [Buf(f"g{i}") for i in range(NFF)]
        PB = [Buf(f"p{i}") for i in range(2)]
        QB = [Buf(f"q{i}") for i in range(3)]
        U = [Buf(f"u{i}") for i in range(3)]
        VT = [Buf(f"vt{i}") for i in range(3)]
        YCP, YCB, SQC = Buf("ycp"), [Buf(f"ycb{i}") for i in range(3)], [Buf(f"sqc{i}") for i in range(3)]
        BN = [Buf(f"bn{i}") for i in range(3)]
        MV = [Buf(f"mv{i}") for i in range(3)]
        RV = [Buf(f"rv{i}") for i in range(3)]
        GUARD = Buf("guard")
        RSSB = Buf("rs_sb")
        CCWB = Buf("ccw")
        DUMMY = Buf("dummy")
        TMPB = [Buf(f"tmp{i}") for i in range(NTMP)]
        SMB = [Buf(f"sm{i}") for i in range(6)]
        DGB = [Buf(f"dg{i}") for i in range(3)]
        BANK = [Buf(f"bank{i}") for i in range(8)]

        st = {"tmp": 0, "ps": 0, "issue": 0, "cons": 0, "rel": 0, "sm": 0, "dg": 0, "dc": 0}
        total_chunks = L * ntiles * NCHUNK

        def TMP():
            i = st["tmp"]
            st["tmp"] = (i + 1) % NTMP
            return tmps[i][:, :], TMPB[i]

        def PS():
            i = st["ps"]
            st["ps"] = (i + 1) % 8
            return banks[i], BANK[i]

        def chunk_src(g):
            l = g // (ntiles * NCHUNK)
            ci = g % NCHUNK
            n = CH
            if ci >= 35 and DSCHED[ci - 35][1] == 2:
                n = 6 * 256
            return wst[l * NCHUNK + ci, :, 0:n], n

        def issue(g):
            s = g % NSLOT
            src, n = chunk_src(g)
            dst = ring[s][:, 0:n]
            P.dma(f"ring{s}", lambda e, dst=dst, src=src: e.dma_start(out=dst, in_=src), writes=[RING[s]])

        def pump():
            while st["issue"] < min(total_chunks, st["rel"] + NSLOT):
                issue(st["issue"])
                st["issue"] += 1

        def consume():
            g = st["cons"]
            pump()
            assert g < st["issue"], "weight ring too shallow for this consumption pattern"
            st["cons"] = g + 1
            s = g % NSLOT
            return hv[s], RING[s]

        def done(n=1):
            st["rel"] += n
            assert st["rel"] <= st["cons"]
            pump()

        P.dma_batch("par", [
            (lambda e: e.dma_start(out=cst[:, :], in_=cst_d), (), [CONST]),
            (lambda e: e.dma_start(out=self_, in_=sel_d), (), SQ),
            (lambda e: e.dma_start(out=gpre[:, :, :], in_=pp_d[:, :, 0:8].rearrange("l p c -> p l c")), (), [CONST]),
        ])
        for t in range(ntiles):
            P.dma(f"x{t}", lambda e, t=t: e.dma_start(out=x[:, :, t * N:(t + 1) * N], in_=x_in[:, :, t * N:(t + 1) * N]),
                  writes=X[t])
        P.op("pool", lambda e: e.memset(ones_bf[:, :], 1.0), writes=[CONST])
        P.op("pool", lambda e: e.memset(neghalf[:, :], -0.5), writes=[CONST])
        P.op("pool", lambda e: e.memset(vTz[:, :, :], 0.0), writes=VT)
        P.op("dve", lambda e: e.tensor_copy(out=sel[:, :], in_=self_), reads=SQ + [CONST], writes=[CONST])

        def SM():
            i = st["sm"]
            st["sm"] = (i + 1) % 6
            return smalls[i], SMB[i]

        def bcast_prep(v3, vb):
            i = st["dg"]
            st["dg"] = (i + 1) % 3
            dg, hl, dgb = dgs[i], hls[i], DGB[i]
            P.op("dve", lambda e: e.tensor_copy(out=hl[:, 0, 0:3], in_=v3), reads=[vb], writes=[dgb])
            P.op("dve", lambda e: e.tensor_tensor(out=hl[:, 1, 0:3], in0=v3, in1=hl[:, 0, 0:3], op=ALU.subtract),
                 reads=[vb, dgb], writes=[dgb])
            P.op("dve", lambda e: e.tensor_tensor(
                out=dg[:, :, :, :], in0=ident.unsqueeze(1).unsqueeze(1).broadcast_to([128, 2, 3, 128]),
                in1=hl[:, :, 0:3].unsqueeze(3).broadcast_to([128, 2, 3, 128]), op=ALU.mult),
                reads=[dgb, CONST], writes=[dgb])
            return dg, dgb

        def bcast_pe(handle, bank=None):
            dg, dgb = handle
            ps2, p2b = bank if bank is not None else PS()
            fns = []
            for c in range(3):
                fns.append(_mm(ps2[:, c * 128:(c + 1) * 128], ones_bf[:, :], dg[:, 0, c, :], True, False))
                fns.append(_mm(ps2[:, c * 128:(c + 1) * 128], ones_bf[:, :], dg[:, 1, c, :], False, True))
            P.group("pe", fns, reads=[dgb, CONST], writes=[p2b])
            return ps2[:, 0:N], p2b

        def bcast(v3, vb):
            return bcast_pe(bcast_prep(v3, vb))

        def stat_finish(ps, pb, n):
            s3, s3b = SM()
            P.op("dve", lambda e: e.tensor_scalar(out=s3[:, 0:3], in0=ps[:, 0:3], scalar1=1.0 / n, scalar2=EPS,
                                                  op0=ALU.mult, op1=ALU.add), reads=[pb], writes=[s3b])
            P.op("pool", lambda e: e.tensor_tensor(out=s3[:, 0:3], in0=s3[:, 0:3], in1=neghalf[:, 0:3], op=ALU.pow),
                 reads=[s3b, CONST], writes=[s3b])
            return bcast_prep(s3[:, 0:3], s3b)

        def stat_prep(srcs, src_bufs, n):
            ps, pb = PS()
            fns = []
            for c in range(3):
                for i, s_ in enumerate(srcs):
                    fns.append(_mm(ps[:, c:c + 1], s_[:, c * 128:(c + 1) * 128], ones_bf[:, 0:1], i == 0, i == len(srcs) - 1))
            P.group("pe", fns, reads=list(src_bufs) + [CONST], writes=[pb])
            return stat_finish(ps, pb, n)

        def stat_rs(srcs, src_bufs, n):
            return bcast_pe(stat_prep(srcs, src_bufs, n))

        def proj(hvv, rb, blk, rhs3, rhs_buf, n, pipelined=False):
            ps, pb = PS()
            fns = [_mm(ps[:, 0:n], hvv[:, kt * 256 + blk * 128: kt * 256 + blk * 128 + 128], rhs3[:, kt, 0:n],
                       kt == 0, kt == KT - 1) for kt in range(KT)]
            rbufs = list(rhs_buf) if isinstance(rhs_buf, list) else [rhs_buf]
            if isinstance(rhs_buf, list) and pipelined:
                for kt in range(KT):
                    P.op("pe", fns[kt], reads=[rb, rhs_buf[kt]], writes=[pb])
            else:
                P.group("pe", fns, reads=[rb] + rbufs, writes=[pb])
            return ps, pb

        def premix_stages(li_, tt_):
            xt_ = x[:, :, tt_ * N:(tt_ + 1) * N]
            xb_ = X[tt_]
            box = {}

            def s1():
                P.op("act", lambda e: e.activation(out=sq[:, :, :], in_=xt_, func=AF.Square), reads=xb_, writes=SQ)

            def s2():
                box["h"] = stat_prep([sq[:, kt, :] for kt in range(KT)], SQ, 1024.0)

            def s3():
                t, tb = bcast_pe(box["h"])
                P.op("act", lambda e: e.activation(out=rs_sb[:, :], in_=t, func=AF.Identity), reads=[tb], writes=[RSSB])
                box["rs"] = (rs_sb[:, :], RSSB)

            def s4(kts=range(KT)):
                t, tb = box["rs"]
                for kt in kts:
                    P.op("dve", lambda e, kt=kt: e.scalar_tensor_tensor(
                        out=h[:, kt, :], in0=xt_[:, kt, :], scalar=gpre[:, li_, kt:kt + 1], in1=t,
                        op0=ALU.mult, op1=ALU.mult), reads=[xb_[kt], tb, CONST], writes=[H])
            return [s1, s2, s3, s4]

        def resid_then_norm(xt, xb, col0):
            t, tb = stat_rs([sq[:, m, :] for m in range(KT)], SQ, 1024.0)
            ps, pb = PS()
            for m in range(KT):
                P.op("dve", lambda e, m=m: e.tensor_tensor(out=y[:, m, :], in0=y[:, m, :], in1=t, op=ALU.mult),
                     reads=[Y[m], tb], writes=[Y[m]])
                P.op("pool", lambda e, m=m: e.tensor_tensor(out=xt[:, m, :], in0=xt[:, m, :], in1=y[:, m, :], op=ALU.add),
                     reads=[Y[m], xb[m]], writes=[xb[m]])
                P.op("act", lambda e, m=m: e.activation(out=sq[:, m, :], in_=xt[:, m, :], func=AF.Square),
                     reads=[xb[m]], writes=[SQ[m]])
                P.group("pe", [_mm(ps[:, c:c + 1], sq[:, m, c * 128:(c + 1) * 128], ones_bf[:, 0:1],
                                   (m == 0 and c == 0), m == KT - 1) for c in range(3)],
                        reads=[SQ[m], CONST], writes=[pb])
            t2, t2b = bcast_pe(stat_finish(ps, pb, 1024.0))
            for kt in range(KT):
                P.op("dve", lambda e, kt=kt: e.scalar_tensor_tensor(
                    out=h2[:, kt, 2:2 + N], in0=xt[:, kt, :], scalar=gs[:, col0 + kt:col0 + kt + 1], in1=t2,
                    op0=ALU.mult, op1=ALU.mult), reads=[xb[kt], t2b, GSB], writes=[H2[kt]])

        def postffn_tail(xt, xb, tt, last_layer, do_stash):
            handle = stat_prep([sq[:, m, :] for m in range(KT)], SQ, 1024.0)

            def fin():
                t, tb = bcast_pe(handle)
                for m in range(KT):
                    P.op("dve", lambda e, m=m: e.tensor_tensor(out=y[:, m, :], in0=y[:, m, :], in1=t, op=ALU.mult),
                         reads=[Y[m], tb], writes=[Y[m]])
                    P.op("pool", lambda e, m=m: e.tensor_tensor(out=xt[:, m, :], in0=xt[:, m, :], in1=y[:, m, :], op=ALU.add),
                         reads=[Y[m], xb[m]], writes=[xb[m]])
                if do_stash:
                    P.op("pool", lambda e: e.tensor_copy(out=h2[:, :, 0:2], in_=h2[:, :, N:N + 2]), reads=H2, writes=H2)
                if last_layer:
                    P.dma("out", lambda e: e.dma_start(out=y_out[:, :, tt * N:(tt + 1) * N],
                                                       in_=x[:, :, tt * N:(tt + 1) * N]), reads=xb)
            return fin

        def guard_switch():
            P.op("pool", lambda e: e.memset(dummy[:, :], 0.0), writes=[GUARD, DUMMY])

        units = [(a, b) for a in range(L) for b in range(ntiles)]
        for f_ in premix_stages(0, 0):
            f_()
        pend = {"tail": None}
        for li in range(L):
            P.dma_batch("par", [
                (lambda e, li=li: e.dma_start(out=pp[:, :], in_=pp_d[li]), (), [PPB]),
                (lambda e, li=li: e.dma_start(out=lng[:, :], in_=lng_d[li]), (), [LNB]),
                (lambda e, li=li: e.dma_start(out=lnb[:, :], in_=lnb_d[li]), (), [LNB]),
                (lambda e, li=li: e.dma_start(out=wTf, in_=wT_d[li].rearrange("p (h t) -> p h t", h=6)), (), SQ),
                (lambda e, li=li: e.dma_start(out=bsf, in_=bs_d[li].rearrange("p (h t) -> p h t", h=3)), (), SQ),
            ])
            for (a, b_, sc) in ((0, 40, 1.0),):
                P.op("dve", lambda e, a=a, b_=b_, sc=sc: e.tensor_scalar(
                    out=gs[:, a:b_], in0=pp[:, a:b_], scalar1=sc, scalar2=None, op0=ALU.mult),
                    reads=[PPB], writes=[GSB])
            P.op("pool", lambda e: e.tensor_tensor(
                out=diagA[:, :, :], in0=ident.unsqueeze(1).broadcast_to([128, 6, 128]),
                in1=pp[:, C_CA:C_CA + 6].unsqueeze(2).broadcast_to([128, 6, 128]), op=ALU.mult),
                reads=[PPB, CONST], writes=[DIAGA])
            P.op("pool", lambda e: e.tensor_tensor(
                out=wT[:, :, :], in0=wTf, in1=maskT.unsqueeze(1).broadcast_to([128, 6, 128]), op=ALU.mult),
                reads=SQ + [CONST], writes=[SGUW])
            P.op("dve", lambda e: e.tensor_copy(out=bsr[:, :, :], in_=bsf), reads=SQ, writes=[SGUW])
            if li == 0:
                for j_ in range(3):
                    P.op("pool", lambda e, j_=j_: e.tensor_tensor(
                        out=diagC[:, j_, :, :], in0=ident.unsqueeze(1).broadcast_to([128, 31, 128]),
                        in1=pp[:, C_CC + 31 * j_:C_CC + 31 * (j_ + 1)].unsqueeze(2).broadcast_to([128, 31, 128]),
                        op=ALU.mult), reads=[PPB, CONST], writes=[DIAGC[j_]])
            P.op("pool", lambda e: e.memset(pbuf[:, :, 0:2], 0.0), writes=PB)
            P.op("pool", lambda e: e.memset(qbuf[:, :, 0:30], 0.0), writes=QB)
            P.op("pool", lambda e: e.memset(h2[:, :, 0:2], 0.0), writes=H2)

            for tt in range(ntiles):
                xt = x[:, :, tt * N:(tt + 1) * N]
                xb = X[tt]
                ui = li * ntiles + tt
                nxt = premix_stages(*units[ui + 1]) if ui + 1 < len(units) else None
                guard_switch()

                def c_conv(j):
                    sl = j
                    ps, pb = PS()
                    P.group("pe", [_mm(ps[:, 0:N], diagC[:, sl, k, :], qbuf[:, j, k:k + N], k == 0, k == 30)
                                   for k in range(31)], reads=[QB[j], DIAGC[sl]], writes=[pb])
                    P.op("act", lambda e: e.activation(
                        out=ycp[:, j, :], in_=ps[:, 0:N], func=AF.Identity, bias=pp[:, C_CB + j:C_CB + j + 1]),
                        reads=[pb, PPB, GUARD], writes=[YCP])
                    P.op("act", lambda e: e.activation(
                        out=sqc[:, j, :], in_=ps[:, 0:N], func=AF.Square, bias=pp[:, C_CB + j:C_CB + j + 1]),
                        reads=[pb, PPB, GUARD], writes=[SQC[j]])
                    P.op("act", lambda e: e.activation(
                        out=ycb[:, j, :], in_=ps[:, 0:N], func=AF.Identity, bias=pp[:, C_CB + j:C_CB + j + 1]),
                        reads=[pb, PPB, GUARD], writes=[YCB[j]])
                    P.op("pool", lambda e: e.tensor_copy(out=qbuf[:, j, 0:30], in_=qbuf[:, j, N:N + 30]),
                         reads=[QB[j]], writes=[QB[j]])

                for j in range(3):
                    cg, cgb = consume()
                    ps, pb = proj(cg, cgb, 0, h, H, N)
                    sg, sgb = TMP()
                    P.op("act", lambda e, ps=ps, sg=sg: e.activation(out=sg, in_=ps[:, 0:N], func=AF.Sigmoid),
                         reads=[pb], writes=[sgb])
                    ps, pb = proj(cg, cgb, 1, h, H, N)
                    done(1)
                    P.op("dve", lambda e, j=j, ps=ps, sg=sg: e.tensor_tensor(
                        out=qbuf[:, j, 30:30 + N], in0=sg, in1=ps[:, 0:N], op=ALU.mult),
                        reads=[pb, sgb], writes=[QB[j]])
                if pend["tail"] is not None:
                    pend["tail"]()
                    pend["tail"] = None
                c_conv(0)
                c0, c0b = consume()
                c1, c1b = consume()
                c2, c2b = consume()
                psv = [PS() for _ in range(3)]
                for c in range(3):
                    ps, pb = psv[c]
                    P.group("pe", [_mm(ps[:, 0:256], h[:, kt, c * 128:(c + 1) * 128], c0[:, kt * 256:(kt + 1) * 256],
                                       kt == 0, kt == KT - 1) for kt in range(KT)], reads=[H, c0b], writes=[pb])
                for c in range(3):
                    ps, pb = psv[c]
                    P.group("pe", [_mm(ps[:, 256:384], h[:, kt, c * 128:(c + 1) * 128], c1[:, kt * 256:kt * 256 + 128],
                                       kt == 0, kt == KT - 1) for kt in range(KT)], reads=[H, c1b], writes=[pb])
                for c in range(3):
                    ps, pb = psv[c]
                    gv, gvb = TMP()
                    P.op("act", lambda e, ps=ps, gv=gv: e.activation(out=gv, in_=ps[:, 0:384], func=AF.Gelu),
                         reads=[pb], writes=[gvb])
                    P.op("dve", lambda e, c=c, gv=gv: e.bn_stats(out=bnst[:, c, :], in_=gv), reads=[gvb], writes=[BN[c]])
                    P.op("dve", lambda e, c=c: e.bn_aggr(out=mv[:, c, :], in_=bnst[:, c, :]), reads=[BN[c]], writes=[MV[c]])
                    P.op("dve", lambda e, c=c: e.tensor_scalar(out=rv[:, c, :], in0=mv[:, c, 1:2], scalar1=EPS,
                                                               scalar2=None, op0=ALU.add), reads=[MV[c]], writes=[RV[c]])
                    P.op("pool", lambda e, c=c: e.tensor_tensor(out=rv[:, c, :], in0=rv[:, c, :], in1=neghalf[:, 0:1],
                                                                op=ALU.pow), reads=[RV[c], CONST], writes=[RV[c]])
                    vn, vnb = TMP()
                    P.op("dve", lambda e, c=c, gv=gv, vn=vn: e.tensor_scalar(
                        out=vn, in0=gv, scalar1=mv[:, c, 0:1], scalar2=rv[:, c, :], op0=ALU.subtract, op1=ALU.mult),
                        reads=[gvb, MV[c], RV[c]], writes=[vnb])
                    P.op("pool", lambda e, vn=vn: e.tensor_tensor(out=vn, in0=vn, in1=lng[:, :], op=ALU.mult),
                         reads=[vnb, LNB], writes=[vnb])
                    base = vTz[:, c, :]
                    dst = bass.AP(base.tensor, base.offset, [list(base.ap[0]), [256, 3], [192, 2], [1, 64]])
                    P.op("pool", lambda e, vn=vn, dst=dst: e.tensor_tensor(
                        out=dst, in0=vn.rearrange("p (j h d) -> p j h d", j=3, h=2),
                        in1=lnb[:, :].rearrange("p (j h d) -> p j h d", j=3, h=2), op=ALU.add),
                        reads=[vnb, LNB], writes=[VT[c]])
                for j, (cv, cb_, blk) in enumerate(((c1, c1b, 1), (c2, c2b, 0), (c2, c2b, 1))):
                    ps, pb = proj(cv, cb_, blk, h, H, N)
                    P.op("act", lambda e, j=j, ps=ps: e.activation(out=u[:, j, :], in_=ps[:, 0:N], func=AF.Gelu),
                         reads=[pb, GUARD], writes=[U[j]])
                done(3)
                c_conv(1)
                c3, c3b = consume()
                c4, c4b = consume()
                c5, c5b = consume()
                a_src = (((c3, c3b, 0), (c3, c3b, 1), (c4, c4b, 0)), ((c4, c4b, 1), (c5, c5b, 0), (c5, c5b, 1)))
                def a_block(j):
                    (cc, ccb, cblk), (cx, cxb, xblk), (cbg, cbgb, bblk) = a_src[j]
                    ps, pb = proj(cc, ccb, cblk, h, H, N)
                    csb, csbb = TMP()
                    P.op("act", lambda e, ps=ps, csb=csb: e.activation(out=csb, in_=ps[:, 0:N], func=AF.Identity),
                         reads=[pb], writes=[csbb])
                    ps, pb = proj(cx, cxb, xblk, h, H, N)
                    P.op("dve", lambda e, j=j, ps=ps, csb=csb: e.tensor_tensor(
                        out=pbuf[:, j, 2:2 + N], in0=csb, in1=ps[:, 0:N], op=ALU.mult),
                        reads=[pb, csbb], writes=[PB[j]])
                    ps, pb = proj(cbg, cbgb, bblk, h, H, N)
                    bsb, bsbb = TMP()
                    P.op("act", lambda e, ps=ps, bsb=bsb: e.activation(out=bsb, in_=ps[:, 0:N], func=AF.Identity),
                         reads=[pb], writes=[bsbb])
                    ps, pb = PS()
                    P.group("pe", [_mm(ps[:, 0:N], diagA[:, j * 3 + k, :], pbuf[:, j, k:k + N], k == 0, k == 2)
                                   for k in range(3)], reads=[PB[j], DIAGA], writes=[pb])
                    P.op("dve", lambda e, j=j, ps=ps, bsb=bsb: e.tensor_tensor(
                        out=y[:, j, :], in0=bsb, in1=ps[:, 0:N], op=ALU.mult), reads=[pb, bsbb], writes=[Y[j]])
                    P.op("pool", lambda e, j=j: e.tensor_copy(out=pbuf[:, j, 0:2], in_=pbuf[:, j, N:N + 2]),
                         reads=[PB[j]], writes=[PB[j]])
                a_block(0)
                a_block(1)
                done(3)
                c_conv(2)
                psl, plb = PS()
                fns = []
                for c in range(3):
                    for j in range(3):
                        fns.append(_mm(psl[:, c:c + 1], ycb[:, j, c * 128:(c + 1) * 128], ones_bf[:, 0:1], j == 0, j == 2))
                for c in range(3):
                    for j in range(3):
                        fns.append(_mm(psl[:, 3 + c:4 + c], sqc[:, j, c * 128:(c + 1) * 128], ones_bf[:, 0:1], j == 0, j == 2))
                P.group("pe", fns, reads=YCB + SQC + [CONST, GUARD], writes=[plb])
                m3, m3b = SM()
                w3, w3b = SM()
                P.op("dve", lambda e, m3=m3, psl=psl: e.tensor_scalar(
                    out=m3[:, 0:6], in0=psl[:, 0:6], scalar1=1.0 / 384.0, scalar2=None, op0=ALU.mult),
                    reads=[plb], writes=[m3b])
                P.op("dve", lambda e, m3=m3, w3=w3: e.tensor_tensor(out=w3[:, 0:3], in0=m3[:, 0:3], in1=m3[:, 0:3], op=ALU.mult),
                     reads=[m3b], writes=[w3b])
                P.op("dve", lambda e, m3=m3, w3=w3: e.scalar_tensor_tensor(
                    out=m3[:, 3:6], in0=m3[:, 3:6], scalar=EPS, in1=w3[:, 0:3], op0=ALU.add, op1=ALU.subtract),
                    reads=[m3b, w3b], writes=[m3b])
                P.op("pool", lambda e, m3=m3: e.tensor_tensor(out=m3[:, 3:6], in0=m3[:, 3:6], in1=neghalf[:, 0:3], op=ALU.pow),
                     reads=[m3b, CONST], writes=[m3b])
                P.op("dve", lambda e, m3=m3, w3=w3: e.tensor_tensor(out=w3[:, 3:6], in0=m3[:, 0:3], in1=m3[:, 3:6], op=ALU.mult),
                     reads=[m3b, w3b], writes=[w3b])
                hA = bcast_prep(m3[:, 3:6], m3b)
                hB = bcast_prep(w3[:, 3:6], w3b)
                for j in range(3):
                    ps, pb = PS()
                    fns = []
                    for c in range(3):
                        o_ = ps[:, c * 128:(c + 1) * 128]
                        fns.append(_mm(o_, vTz[:, c, (2 * j) * 128:(2 * j + 1) * 128], wT[:, 2 * j, :], True, False))
                        fns.append(_mm(o_, vTz[:, c, (2 * j + 1) * 128:(2 * j + 2) * 128], wT[:, 2 * j + 1, :], False, False))
                        fns.append(_mm(o_, sel[0:2, :], bsr[0:2, j, :], False, True))
                    P.group("pe", fns, reads=VT + [SGUW, CONST], writes=[pb])
                    P.op("dve", lambda e, j=j, ps=ps: e.tensor_tensor(
                        out=y[:, 2 + j, :], in0=u[:, j, :], in1=ps[:, 0:N], op=ALU.mult),
                        reads=[pb, U[j], GUARD], writes=[Y[2 + j]])
                rA, rAb = bcast_pe(hA, bank=(psl, plb))
                gh = {}
                for (g_, b0, b1, n_) in (("A", 0, 2, 256.0), ("B", 2, 5, 384.0)):
                    P.op("act", lambda e, b0=b0, b1=b1: e.activation(out=sq[:, b0:b1, :], in_=y[:, b0:b1, :], func=AF.Square),
                         reads=Y[b0:b1], writes=SQ[b0:b1])
                    gh[g_] = stat_prep([sq[:, i, :] for i in range(b0, b1)], SQ[b0:b1], n_)
                rB, rBb = bcast_pe(hB)
                P.op("dve", lambda e, rA=rA: e.tensor_tensor(
                    out=ycp, in0=ycp, in1=rA.unsqueeze(1).broadcast_to([128, 3, N]), op=ALU.mult),
                    reads=[YCP, rAb, GUARD], writes=[YCP])
                P.op("dve", lambda e, rB=rB: e.tensor_tensor(
                    out=ycp, in0=ycp, in1=rB.unsqueeze(1).broadcast_to([128, 3, N]), op=ALU.subtract),
                    reads=[YCP, rBb, GUARD], writes=[YCP])
                for j in range(3):
                    P.op("act", lambda e, j=j: e.activation(
                        out=y[:, 5 + j, :], in_=ycp[:, j, :], func=AF.Silu,
                        scale=pp[:, C_LG + j:C_LG + j + 1], bias=pp[:, C_LB + j:C_LB + j + 1]),
                        reads=[YCP, PPB, GUARD], writes=[Y[5 + j]])
                def ynorm_ops(t, tb, b0, b1):
                    for i in range(b0, b1):
                        P.op("dve", lambda e, i=i: e.scalar_tensor_tensor(
                            out=h[:, i, :], in0=y[:, i, :], scalar=gs[:, C_GRP + i:C_GRP + i + 1], in1=t,
                            op0=ALU.mult, op1=ALU.mult), reads=[Y[i], tb, GSB], writes=[H])

                t, tb = bcast_pe(gh["A"])
                ynorm_ops(t, tb, 0, 2)
                t, tb = bcast_pe(gh["B"])
                ynorm_ops(t, tb, 2, 5)
                P.op("act", lambda e: e.activation(out=sq[:, 5:8, :], in_=y[:, 5:8, :], func=AF.Square),
                     reads=Y[5:8], writes=SQ[5:8])
                hC = stat_prep([sq[:, i, :] for i in range(5, 8)], SQ[5:8], 384.0)
                early = []
                for mp in range(2):
                    cw, cwb = consume()
                    for half in range(2):
                        ps, pb = PS()
                        P.group("pe", [_mm(ps[:, 0:N], cw[:, kt * 256 + half * 128: kt * 256 + half * 128 + 128], h[:, kt, :],
                                           kt == 0, False) for kt in range(5)], reads=[cwb, H], writes=[pb])
                        early.append((ps, pb, cw, cwb, half))
                t, tb = bcast_pe(hC)
                ynorm_ops(t, tb, 5, 8)

                def out_evac(m, ps, pb):
                    P.op("act", lambda e: e.activation(out=sq[:, m, :], in_=ps[:, 0:N], func=AF.Square),
                         reads=[pb], writes=[SQ[m]])
                    P.op("act", lambda e: e.activation(
                        out=y[:, m, :], in_=ps[:, 0:N], func=AF.Identity, scale=gs[:, C_POST + m:C_POST + m + 1]),
                        reads=[pb, GSB], writes=[Y[m]])

                for m, (ps, pb, cw, cwb, half) in enumerate(early):
                    P.group("pe", [_mm(ps[:, 0:N], cw[:, kt * 256 + half * 128: kt * 256 + half * 128 + 128], h[:, kt, :],
                                       False, kt == KT - 1) for kt in range(5, KT)], reads=[cwb, H], writes=[pb])
                    out_evac(m, ps, pb)
                done(2)
                for mp in range(2, 4):
                    cw, cwb = consume()
                    for half in range(2):
                        m = 2 * mp + half
                        ps, pb = proj(cw, cwb, half, h, H, N)
                        out_evac(m, ps, pb)
                    done(1)
                resid_then_norm(xt, xb, C_PFF)
                guard_switch()
                for j in range(NFF):
                    if nxt is not None:
                        if j in (2, 5, 9):
                            nxt[(2, 5, 9).index(j)]()
                        elif 11 <= j < 19:
                            nxt[3]([j - 11])
                    cw, cwb = consume()
                    psg, pgb = proj(cw, cwb, 0, h2, H2, N + 2, pipelined=(j == 0))
                    psv_, pvb = proj(cw, cwb, 1, h2, H2, N + 2)
                    done(1)
                    tg, tgb = TMP()
                    tv, tvb = TMP()
                    cg0 = C_FG + 3 * j
                    cv0 = C_FV + 3 * j
                    P.op("act", lambda e, tg=tg, psg=psg, cg0=cg0: e.activation(
                        out=tg, in_=psg[:, 0:N], func=AF.Identity, scale=pp[:, cg0:cg0 + 1]),
                        reads=[pgb, PPB], writes=[tgb])
                    P.op("act", lambda e, tv=tv, psv_=psv_, cv0=cv0: e.activation(
                        out=tv, in_=psv_[:, 0:N], func=AF.Identity, scale=pp[:, cv0:cv0 + 1]),
                        reads=[pvb, PPB], writes=[tvb])
                    for k in (1, 2):
                        P.op("dve", lambda e, tg=tg, psg=psg, cg0=cg0, k=k: e.scalar_tensor_tensor(
                            out=tg, in0=psg[:, k:k + N], scalar=pp[:, cg0 + k:cg0 + k + 1], in1=tg,
                            op0=ALU.mult, op1=ALU.add), reads=[pgb, tgb, PPB], writes=[tgb])
                        P.op("dve", lambda e, tv=tv, psv_=psv_, cv0=cv0, k=k: e.scalar_tensor_tensor(
                            out=tv, in0=psv_[:, k:k + N], scalar=pp[:, cv0 + k:cv0 + k + 1], in1=tv,
                            op0=ALU.mult, op1=ALU.add), reads=[pvb, tvb, PPB], writes=[tvb])
                    sgt, sgtb = TMP()
                    P.op("act", lambda e, tg=tg, sgt=sgt: e.activation(out=sgt, in_=tg, func=AF.Silu),
                         reads=[tgb], writes=[sgtb])
                    P.op("pool", lambda e, j=j, sgt=sgt, tv=tv: e.tensor_tensor(
                        out=gated[:, j, :], in0=sgt, in1=tv, op=ALU.mult),
                        reads=[sgtb, tvb, GUARD], writes=[G[j]])
                if tt == ntiles - 1 and li + 1 < L:
                    P.dma("ccw", lambda e, li=li: e.dma_start(out=ccw[:, :], in_=pp_d[li + 1][:, C_CC:C_CC + 93]), writes=[CCWB])
                    for j_ in range(3):
                        P.op("pool", lambda e, j_=j_: e.tensor_tensor(
                            out=diagC[:, j_, :, :], in0=ident.unsqueeze(1).broadcast_to([128, 31, 128]),
                            in1=ccw[:, 31 * j_:31 * (j_ + 1)].unsqueeze(2).broadcast_to([128, 31, 128]),
                            op=ALU.mult), reads=[CCWB, CONST], writes=[DIAGC[j_]])
                dbanks = {}
                for (mp, part) in DSCHED:
                    if mp not in dbanks:
                        dbanks[mp] = (PS(), PS())
                    (ps0, p0b), (ps1, p1b) = dbanks[mp]
                    cw, cwb = consume()
                    nk = 8 if part < 2 else 6
                    fns = []
                    for kk in range(nk):
                        kt = part * 8 + kk
                        fns.append(_mm(ps0[:, 0:N], cw[:, kk * 256:kk * 256 + 128], gated[:, kt, :], kt == 0, kt == NFF - 1))
                        fns.append(_mm(ps1[:, 0:N], cw[:, kk * 256 + 128:kk * 256 + 256], gated[:, kt, :], kt == 0, kt == NFF - 1))
                    P.group("pe", fns, reads=[cwb, GUARD] + G[part * 8:part * 8 + nk], writes=[p0b, p1b])
                    done(1)
                    if part == 2:
                        for half, (ps, pb) in enumerate(((ps0, p0b), (ps1, p1b))):
                            m = 2 * mp + half
                            P.op("act", lambda e, m=m, ps=ps: e.activation(out=sq[:, m, :], in_=ps[:, 0:N], func=AF.Square),
                                 reads=[pb], writes=[SQ[m]])
                            P.op("act", lambda e, m=m, ps=ps: e.activation(
                                out=y[:, m, :], in_=ps[:, 0:N], func=AF.Identity, scale=gs[:, C_OFF + m:C_OFF + m + 1]),
                                reads=[pb, GSB], writes=[Y[m]])
                pend["tail"] = postffn_tail(xt, xb, tt, li == L - 1, tt < ntiles - 1)
        pend["tail"]()
        if debug:
            allb = [H, GUARD, SGUW, GSB] + H2 + Y + G + QB + PB + VT
            P.dma_batch("out", [
                (lambda e: e.dma_start(out=dbg_d["d_h"], in_=h[:, :, :]), allb, ()),
                (lambda e: e.dma_start(out=dbg_d["d_h2"], in_=h2[:, :, :]), allb, ()),
                (lambda e: e.dma_start(out=dbg_d["d_y"], in_=y[:, :, :]), allb, ()),
                (lambda e: e.dma_start(out=dbg_d["d_R"], in_=R[:, :]), allb, ()),
                (lambda e: e.dma_start(out=dbg_d["d_q"], in_=qbuf[:, :, :]), allb, ()),
                (lambda e: e.dma_start(out=dbg_d["d_p"], in_=pbuf[:, :, :]), allb, ()),
                (lambda e: e.dma_start(out=dbg_d["d_vT"], in_=vTz[:, :, :]), allb, ()),
                (lambda e: e.dma_start(out=dbg_d["d_gs"], in_=gs[:, :]), allb, ()),
                (lambda e: e.dma_start(out=dbg_d["d_wT"], in_=wT[:, :, :]), allb, ()),
            ])
        P.ops["sp"].append(("wait", "out", P.dmacnt["out"]))

        global LAST_PELOG
        LAST_PELOG = P.pelog
        semkeys = set(P.ENGS) | set(P.dmacnt.keys())
        sems = {k: es.enter_context(nc.semaphore(f"s_{k}")) for k in sorted(semkeys)}
        block = es.enter_context(nc.Block())

        def emit(e, eng):
            mysem = sems[eng]
            for o in P.ops[eng]:
                if o[0] == "wait":
                    e.wait_ge(sems[o[1]], o[2])
                elif o[0] == "ins":
                    ins = o[1](e)
                    if o[2]:
                        ins.then_inc(mysem, 1)
                else:
                    o[1](e).then_inc(sems[o[2]], 16)

        @block.sync
        def _(e):
            emit(e, "sp")

        @block.tensor
        def _(e):
            emit(e, "pe")

        @block.scalar
        def _(e):
            emit(e, "act")

        @block.vector
        def _(e):
            emit(e, "dve")

        @block.gpsimd
        def _(e):
            emit(e, "pool")
    return nc


W_IN_BLOCKS = [1920, 1536, 2048, 1664, 2176, 1792,
               1152, 1280, 1408, 768, 896, 1024,
               256, 512, 0, 384, 640, 128]


def _fm(v):
    return np.ascontiguousarray(v.reshape(-1, 128).T)


def prep_layer_weights(inp, l):
    chunks = np.zeros((NCHUNK, 128, CH), np.float32)
    w_in = inp["w_in"][l]
    cols = np.concatenate([np.arange(c, c + 128) for c in W_IN_BLOCKS])
    wi = w_in[:, cols].reshape(KT, 128, 9, 256)
    chunks[0:9] = wi.transpose(2, 1, 0, 3).reshape(9, 128, CH)
    wo = inp["w_out"][l].reshape(KT, 128, 4, 256)
    chunks[9:13] = wo.transpose(2, 1, 0, 3).reshape(4, 128, CH)
    wu = inp["w_up"][l]
    wg = wu[:, :DFF].reshape(KT, 128, NFF, 128)
    wv = wu[:, DFF:].reshape(KT, 128, NFF, 128)
    wgv = np.concatenate([wg, wv], axis=3)
    chunks[13:35] = wgv.transpose(2, 1, 0, 3).reshape(NFF, 128, CH)
    wd = inp["w_down"][l].reshape(NFF, 128, 4, 256)
    ci = 35
    for (mp, part) in DSCHED:
        nk = 8 if part < 2 else 6
        blk = wd[part * 8:part * 8 + nk, :, mp, :]
        chunks[ci, :, 0:nk * 256] = blk.transpose(1, 0, 2).reshape(128, nk * 256)
        ci += 1
    return chunks


def prep_layer_params(inp, l):
    pp = np.zeros((128, NP), np.float32)
    pp[:, C_PRE:C_PRE + 8] = _fm(inp["pre_mix_g"][l])
    pp[:, C_GRP:C_GRP + 8] = _fm(inp["grp_norm_g"][l])
    pp[:, C_POST:C_POST + 8] = _fm(inp["post_mix_g"][l])
    pp[:, C_PFF:C_PFF + 8] = _fm(inp["pre_ffn_g"][l])
    pp[:, C_OFF:C_OFF + 8] = _fm(inp["post_ffn_g"][l])
    ca = inp["conv_a_w"][l]
    for j in range(2):
        for k in range(3):
            pp[:, C_CA + 3 * j + k] = ca[k, j * 128:(j + 1) * 128]
    cc = inp["conv_c_w"][l]
    for j in range(3):
        pp[:, C_CC + 31 * j:C_CC + 31 * (j + 1)] = cc[:, j * 128:(j + 1) * 128].T
    pp[:, C_CB:C_CB + 3] = _fm(inp["conv_c_b"][l])
    pp[:, C_LG:C_LG + 3] = _fm(inp["conv_ln_g"][l])
    pp[:, C_LB:C_LB + 3] = _fm(inp["conv_ln_b"][l])
    fw = inp["ffn_conv_w"][l]
    for j in range(NFF):
        pp[:, C_FG + 3 * j:C_FG + 3 * j + 3] = fw[:, j * 128:(j + 1) * 128].T
        pp[:, C_FV + 3 * j:C_FV + 3 * j + 3] = fw[:, DFF + j * 128:DFF + (j + 1) * 128].T
    lng = np.ascontiguousarray(np.broadcast_to(inp["sgu_ln_g"][l][None, :], (128, 384))).astype(np.float32)
    lnb = np.ascontiguousarray(np.broadcast_to(inp["sgu_ln_b"][l][None, :], (128, 384))).astype(np.float32)
    wT = np.ascontiguousarray(inp["sgu_w"][l].transpose(2, 0, 1)).reshape(128, 6 * 128)
    sb = inp["sgu_b"][l]
    bs = np.ascontiguousarray(sb.reshape(3, 2, 128).transpose(1, 0, 2)).reshape(2, 3 * 128)
    return pp, lng, lnb, wT, bs


def consts():
    cst = np.zeros((128, 256), np.float32)
    cst[:, 0:128] = np.eye(128, dtype=np.float32)
    s = np.arange(128)
    cst[:, 128:256] = (s[:, None] <= s[None, :]).astype(np.float32)
    sel = np.zeros((2, 128), np.float32)
    sel[0, 0:64] = 1.0
    sel[1, 64:128] = 1.0
    return cst, sel


def core_tokens(c):
    b, half = divmod(c, 2)
    start = 0 if half == 0 else SEQ - T
    return b, half, start


def shard_x(xfull):
    outs = []
    for c in range(8):
        b, half, start = core_tokens(c)
        xs = xfull[b, start:start + T, :]
        outs.append(np.ascontiguousarray(xs.T.reshape(KT, 128, T).transpose(1, 0, 2)))
    return outs


def unshard_y(ys, out):
    for c in range(8):
        b, half, start = core_tokens(c)
        yt = ys[c].transpose(1, 0, 2).reshape(D, T).T
        if half == 0:
            out[b, 0:SEQ // 2] = yt[0:SEQ // 2]
        else:
            out[b, SEQ // 2:] = yt[HALO:]


_PROG_CACHE = {}
LAST_PELOG = None


def run_layers(xs, inp, layers):
    L = len(layers)
    if L not in _PROG_CACHE:
        _PROG_CACHE[L] = build_program(L)
    nc = _PROG_CACHE[L]
    wst = np.concatenate([prep_layer_weights(inp, l) for l in layers], axis=0)
    prm = [prep_layer_params(inp, l) for l in layers]
    pp = np.stack([p[0] for p in prm])
    lng = np.stack([p[1] for p in prm])
    lnb = np.stack([p[2] for p in prm])
    wT = np.stack([p[3] for p in prm])
    bs = np.stack([p[4] for p in prm])
    cst, sel = consts()
    in_maps = [{"x_in": xs[c], "wst": wst, "pp": pp, "lng": lng, "lnb": lnb, "sguwT": wT, "sgub": bs,
                "cst": cst, "sel": sel} for c in range(8)]
    res = run_bass_kernel_spmd(nc, in_maps, core_ids=list(range(8)))
    return [np.asarray(r["y_out"]) for r in res.results]


FUSED = True


def kernel(**inputs):
    inp = {k: np.asarray(v) for k, v in inputs.items()}
    xs = shard_x(inp["x"].astype(np.float32, copy=False))
    if FUSED:
        ys = run_layers(xs, inp, list(range(DEPTH)))
    else:
        ys = xs
        for l in range(DEPTH):
            ys = run_layers(ys, inp, [l])
    out = np.empty((BATCH, SEQ, D), np.float32)
    unshard_y(ys, out)
    return out
```
